# Optimizing a Trainium2 kernel written in Bass

```python
import jax, jax.numpy as jnp
from jax import lax
import numpy as np

D_MODEL = 1024
BATCH = 8
SEQ = 2048
DEPTH = 1
DEC_BATCH = 128
DEC_SEQ = 1
PAST_LEN = 16384
PAGE_SIZE = 128

N_HEADS = 8
HEAD_K = D_MODEL // N_HEADS
HEAD_V = D_MODEL // N_HEADS
D_HGRN = N_HEADS * HEAD_K
D_CONV = D_MODEL
CONV_W = 3
N_META = 16
CHUNK = 32
EPS = 1e-6
SPLIT_WIDTHS = (D_HGRN, D_HGRN, N_HEADS * HEAD_V, N_HEADS * HEAD_V,
                D_CONV, D_CONV, D_CONV, D_CONV, D_MODEL, D_MODEL)
D_IN = sum(SPLIT_WIDTHS)

kernel_name = "hgrn2_shortconv_gated_hybrid_step"


def rmsnorm(x, w):
    xf = x.astype(jnp.float32)
    var = jnp.mean(xf * xf, axis=-1, keepdims=True)
    return (xf * lax.rsqrt(var + EPS) * w.astype(jnp.float32)).astype(x.dtype)


def gla_chunk(S, q, k, v, g):
    C = q.shape[2]
    G = jnp.cumsum(g, axis=2)
    o_inter = jnp.einsum('nhtk,nhkv->nhtv', q * jnp.exp(G), S)
    causal = jnp.tril(jnp.ones((C, C), dtype=bool))[None, None, :, :, None]
    diff = G[:, :, :, None, :] - G[:, :, None, :, :]
    decay = jnp.where(causal, jnp.exp(jnp.where(causal, diff, 0.0)), 0.0)
    scores = jnp.einsum('nhtk,nhsk,nhtsk->nhts', q, k, decay)
    o_intra = jnp.einsum('nhts,nhsv->nhtv', scores, v)
    G_last = G[:, :, -1:, :]
    S_new = jnp.exp(G_last[:, :, 0, :])[..., None] * S + jnp.einsum(
        'nhsk,nhsv->nhkv', k * jnp.exp(G_last - G), v)
    return S_new, o_inter + o_intra


def hgrn_mix(q, k, v, g, S0, lead, chunk):
    N, H, L, _ = q.shape
    S = S0
    outs = []
    if lead > 0:
        S, o = gla_chunk(S, q[:, :, :lead], k[:, :, :lead], v[:, :, :lead], g[:, :, :lead])
        outs.append(o)
    rest = L - lead
    n = rest // chunk

    def to_blocks(a):
        a = a[:, :, lead:]
        return a.reshape(N, H, n, chunk, a.shape[-1]).transpose(2, 0, 1, 3, 4)

    S, oc = lax.scan(lambda s, xs: gla_chunk(s, *xs), S,
                     (to_blocks(q), to_blocks(k), to_blocks(v), to_blocks(g)))
    outs.append(oc.transpose(1, 2, 0, 3, 4).reshape(N, H, rest, HEAD_V))
    return jnp.concatenate(outs, axis=2), S


def mixer_layer(h, conv_ctx, S0, lb, w_in, norm_pre, norm_post, hgrn_norm, conv_w,
                w_a, w_b, w_o, lead, chunk):
    N, L, _ = h.shape
    xn = rmsnorm(h, norm_pre)
    proj = xn @ w_in
    idx = np.cumsum(SPLIT_WIDTHS)[:-1].tolist()
    q, fr, i_in, z_a, b_g, c_g, h_c, z_b, gate_a, gate_b = jnp.split(proj, idx, axis=-1)

    lbf = lb.astype(jnp.float32)
    f = lbf + (1.0 - lbf) * jax.nn.sigmoid(fr.astype(jnp.float32))
    log_f = jnp.log(f)
    k_in = 1.0 - f
    qa = jax.nn.silu(q.astype(jnp.float32))

    def heads(a):
        return a.reshape(N, L, N_HEADS, -1).transpose(0, 2, 1, 3)

    o, S_new = hgrn_mix(heads(qa), heads(k_in), heads(i_in.astype(jnp.float32)),
                        heads(log_f), S0.astype(jnp.float32), lead, chunk)
    o = o.transpose(0, 2, 1, 3)
    o = o * lax.rsqrt(jnp.mean(o * o, axis=-1, keepdims=True) + EPS) * hgrn_norm.astype(jnp.float32)
    o = o.reshape(N, L, N_HEADS * HEAD_V).astype(h.dtype) * jax.nn.silu(z_a)
    y_a = o @ w_a

    u = c_g * h_c
    u_ext = jnp.concatenate([conv_ctx.astype(u.dtype), u], axis=1)
    conv = sum(conv_w[j] * u_ext[:, j:j + L] for j in range(CONV_W))
    y_b = (b_g * conv * jax.nn.silu(z_b)) @ w_b
    new_ctx = u_ext[:, -(CONV_W - 1):]

    merged = jax.nn.sigmoid(gate_a) * y_a + jax.nn.sigmoid(gate_b) * y_b
    out = merged @ w_o
    h = h + rmsnorm(out, norm_post)
    return h, new_ctx, S_new


def setup_inputs(seed: int = 0) -> dict:
    key = jax.random.key(seed)
    ks = jax.random.split(key, 16)
    f32 = jnp.float32
    nrm = lambda k, s, sc: jax.random.normal(k, s, f32) * sc
    return {
        "x_prompt": nrm(ks[0], (BATCH, SEQ, D_MODEL), 1.0),
        "x_sample": nrm(ks[1], (DEC_BATCH, DEC_SEQ, D_MODEL), 1.0),
        "state_hgrn": nrm(ks[2], (DEPTH, DEC_BATCH, N_HEADS, HEAD_K, HEAD_V), 0.5),
        "state_conv": nrm(ks[3], (DEPTH, DEC_BATCH, CONV_W - 1, D_CONV), 0.5),
        "meta_tokens": nrm(ks[4], (N_META, D_MODEL), 1.0),
        "w_in": nrm(ks[5], (DEPTH, D_MODEL, D_IN), D_MODEL ** -0.5),
        "norm_pre": 1.0 + nrm(ks[6], (DEPTH, D_MODEL), 0.02),
        "norm_post": 1.0 + nrm(ks[7], (DEPTH, D_MODEL), 0.02),
        "lb_logits": nrm(ks[8], (DEPTH + 1, D_HGRN), 0.5),
        "hgrn_norm": 1.0 + nrm(ks[9], (DEPTH, HEAD_V), 0.02),
        "conv_w": nrm(ks[10], (DEPTH, CONV_W, D_CONV), CONV_W ** -0.5),
        "w_a": nrm(ks[11], (DEPTH, N_HEADS * HEAD_V, D_MODEL), (N_HEADS * HEAD_V) ** -0.5),
        "w_b": nrm(ks[12], (DEPTH, D_CONV, D_MODEL), D_CONV ** -0.5),
        "w_o": nrm(ks[13], (DEPTH, D_MODEL, D_MODEL), D_MODEL ** -0.5),
    }


def reference(x_prompt, x_sample, state_hgrn, state_conv, meta_tokens, w_in, norm_pre,
              norm_post, lb_logits, hgrn_norm, conv_w, w_a, w_b, w_o):
    lower_bounds = jnp.cumsum(jax.nn.softmax(lb_logits.astype(jnp.float32), axis=0), axis=0)

    hp = jnp.concatenate([jnp.broadcast_to(meta_tokens.astype(x_prompt.dtype)[None],
                                           (BATCH, N_META, D_MODEL)), x_prompt], axis=1)
    conv0 = jnp.zeros((BATCH, CONV_W - 1, D_CONV), x_prompt.dtype)
    S0 = jnp.zeros((BATCH, N_HEADS, HEAD_K, HEAD_V), jnp.float32)
    hs = x_sample
    hgrn_p, hgrn_s, conv_p, conv_s = [], [], [], []
    for l in range(DEPTH):
        hp, cp, sp = mixer_layer(hp, conv0, S0, lower_bounds[l], w_in[l], norm_pre[l], norm_post[l],
                                 hgrn_norm[l], conv_w[l], w_a[l], w_b[l], w_o[l], N_META, CHUNK)
        hs, cs, ss = mixer_layer(hs, state_conv[l], state_hgrn[l], lower_bounds[l], w_in[l],
                                 norm_pre[l], norm_post[l], hgrn_norm[l], conv_w[l], w_a[l],
                                 w_b[l], w_o[l], 0, DEC_SEQ)
        hgrn_p.append(sp); hgrn_s.append(ss); conv_p.append(cp); conv_s.append(cs)
    y_prompt = hp[:, N_META:]
    y_sample = hs
    return (y_prompt, y_sample, jnp.stack(hgrn_p), jnp.stack(hgrn_s),
            jnp.stack(conv_p), jnp.stack(conv_s))
```

```python
import contextlib
import numpy as np
import concourse.bass as bass
import concourse.mybir as mybir
from concourse.bass_utils import run_bass_kernel_spmd

F32 = mybir.dt.float32
BF16 = mybir.dt.bfloat16
ALU = mybir.AluOpType
AF = mybir.ActivationFunctionType

NCORES = 8
D = 1024
SEQ = 2048
NS = 16
NMETA = 16
T = NMETA + NS + SEQ
NBLK = 5
BL = T // NBLK
NH = 8
EPS = 1e-6
NGRP = 16
NCH = 1 + NS + 64


class Res:
    __slots__ = ("name", "w", "r")

    def __init__(self, name):
        self.name = name
        self.w = None
        self.r = []

    def inherit(self, *olds):
        for o in olds:
            if o.w is not None:
                self.r.append(o.w)
            self.r.extend(o.r)


class Op:
    __slots__ = ("eng", "fn", "deps", "idx", "dma_sem", "dma_val", "signaled", "is_dma")


class Sched:
    ENG = ("pe", "act", "dve", "pool", "sp")

    def __init__(self, nc):
        self.nc = nc
        self.e = {"pe": nc.tensor, "act": nc.scalar, "dve": nc.vector, "pool": nc.gpsimd, "sp": nc.sync}
        self.ops = {k: [] for k in self.ENG}
        self.dma_cnt = {}
        self.dma_sems = {}
        self.eng_sems = {}

    def _collect(self, reads, writes):
        deps = []
        for r in reads:
            if r.w is not None:
                deps.append(r.w)
        for w in writes:
            if w.w is not None:
                deps.append(w.w)
            deps.extend(w.r)
        return deps

    def _finish(self, o, ev, reads, writes):
        self.ops[o.eng].append(o)
        for r in reads:
            r.r.append(ev)
        for w in writes:
            w.w = ev
            w.r = []

    def op(self, eng, fn, reads=(), writes=()):
        o = Op()
        o.eng = eng
        o.fn = fn
        o.deps = self._collect(reads, writes)
        o.idx = len(self.ops[eng])
        o.is_dma = False
        o.signaled = False
        o.dma_sem = None
        self._finish(o, ("eng", eng, o.idx), reads, writes)
        return o

    def dma(self, eng, fn, sem, reads=(), writes=(), n=1):
        o = Op()
        o.eng = eng
        o.fn = fn
        o.deps = self._collect(reads, writes)
        o.idx = len(self.ops[eng])
        o.is_dma = True
        o.signaled = False
        o.dma_sem = sem
        self.dma_cnt[sem] = self.dma_cnt.get(sem, 0) + 16 * n
        o.dma_val = self.dma_cnt[sem]
        ev = ("dma", sem, o.dma_val)
        self._finish(o, ev, reads, writes)
        return ev

    def emit(self, stack, final_waits=()):
        nc = self.nc
        for k in self.ENG:
            for o in self.ops[k]:
                for d in o.deps:
                    if d[0] == "eng":
                        if d[1] == k and k == "pe":
                            continue
                        self.ops[d[1]][d[2]].signaled = True
        val = {}
        for k in self.ENG:
            c = 0
            v = []
            for o in self.ops[k]:
                if o.signaled:
                    c += 1
                v.append(c)
            val[k] = v
        for k in self.ENG:
            self.eng_sems[k] = stack.enter_context(nc.semaphore("s_" + k))
        for s in self.dma_cnt:
            self.dma_sems[s] = stack.enter_context(nc.semaphore("d_" + s))
        for k in self.ENG:
            eng = self.e[k]
            seen = {}
            for o in self.ops[k]:
                need = {}
                for d in o.deps:
                    if d[0] == "eng":
                        if d[1] == k and k == "pe":
                            continue
                        key = ("eng", d[1])
                        v = val[d[1]][d[2]]
                    else:
                        key = ("dma", d[1])
                        v = d[2]
                    if seen.get(key, 0) >= v:
                        continue
                    if need.get(key, 0) < v:
                        need[key] = v
                for key, v in need.items():
                    sem = self.eng_sems[key[1]] if key[0] == "eng" else self.dma_sems[key[1]]
                    eng.wait_ge(sem, v)
                    seen[key] = v
                ins = o.fn(eng)
                if o.is_dma:
                    if not isinstance(ins, (list, tuple)):
                        ins = [ins]
                    for i_ in ins:
                        i_.then_inc(self.dma_sems[o.dma_sem], 16)
                elif o.signaled:
                    ins.then_inc(self.eng_sems[k], 1)
        for d in final_waits:
            nc.sync.wait_ge(self.dma_sems[d[1]], d[2])


def build_program():
    nc = bass.Bass("TRN2", target_bir_lowering=False)
    S = Sched(nc)
    st = contextlib.ExitStack()

    def din(name, shape):
        return nc.dram_tensor(name, shape, F32, kind="ExternalInput").ap()

    def dout(name, shape):
        return nc.dram_tensor(name, shape, F32, kind="ExternalOutput").ap()

    x_prompt = din("x_prompt", [SEQ, D])
    x_sample = din("x_sample", [NS, D])
    state_hgrn = din("state_hgrn", [NS, NH, 128, 128])
    state_conv = din("state_conv", [NS, 2, D])
    meta_tokens = din("meta_tokens", [NMETA, D])
    w_in = din("w_in", [D, 10 * D])
    norm_pre = din("norm_pre", [1, D])
    norm_post = din("norm_post", [1, D])
    lb_logits = din("lb_logits", [2, D])
    hgrn_norm = din("hgrn_norm", [1, 128])
    conv_w = din("conv_w", [3, D])
    w_a = din("w_a", [D, D])
    w_b = din("w_b", [D, D])
    w_o = din("w_o", [D, D])

    y_prompt = dout("y_prompt", [SEQ, D])
    y_sample = dout("y_sample", [NS, D])
    nh_prompt = dout("nh_prompt", [NH, 128, 128])
    nh_sample = dout("nh_sample", [NS, NH, 128, 128])
    nc_prompt = dout("nc_prompt", [2, D])
    nc_sample = dout("nc_sample", [NS, 2, D])

    def sb(name, shape, dt):
        return st.enter_context(nc.sbuf_tensor(name, shape, dt))

    import os as _os0
    DBG_S = nc.dram_tensor("dbg_S", [NH, 64, 128, 128], F32, kind="ExternalOutput").ap() \
        if _os0.environ.get("KDEBUG") else None

    def ps(name, shape, dt):
        return st.enter_context(nc.psum_tensor(name, shape, dt))

    out_events = []

    xnT = sb("xnT", [128, 8, T], BF16)
    oz = sb("oz", [128, 8, T], BF16)
    R_xnT = [Res("xnT%d" % b) for b in range(NBLK)]
    R_xnT_tile = [Res("xnTt%d" % i) for i in range(17)]
    R_oz = [Res("oz%d" % h) for h in range(NH)]
    HB = {}
    R_HB = {}
    for kind in ("q", "k", "kh", "z"):
        for par in range(2):
            HB[(kind, par)] = sb("hb_%s%d" % (kind, par), [128, T], BF16)
            for b in range(NBLK):
                R_HB[(kind, par, b)] = Res("hb_%s%d_%d" % (kind, par, b))
    merged_bufs = [HB[(kind, par)] for par in range(2) for kind in ("q", "k", "kh", "z")]
    merged_res = [[R_HB[(kind, par, b)] for b in range(NBLK)] for par in range(2) for kind in ("q", "k", "kh", "z")]

    R2_BYTES = 55 * 1024
    r2 = sb("r2", [128, R2_BYTES // 2], BF16)
    r2f = r2[:].bitcast(F32)
    r2_off = [0]

    def r2_bf(nelem, shape=None):
        o = r2_off[0]
        assert o % 4 == 0
        r2_off[0] += ((nelem * 2 + 31) // 32) * 32
        assert r2_off[0] <= R2_BYTES, r2_off[0]
        return (o // 2, nelem)

    def r2_f(nelem):
        o = r2_off[0]
        r2_off[0] += ((nelem * 4 + 31) // 32) * 32
        assert r2_off[0] <= R2_BYTES, r2_off[0]
        return (o // 4, nelem)

    def v_bf(loc):
        return r2[:, loc[0]:loc[0] + loc[1]]

    def v_f(loc):
        return r2f[:, loc[0]:loc[0] + loc[1]]

    a2_res = []

    def res2(name):
        r = Res(name)
        a2_res.append(r)
        return r

    vtm_loc = [r2_bf(17 * 128) for _ in range(2)]
    vtm = [v_bf(l).rearrange("p (t v) -> p t v", v=128) for l in vtm_loc]
    R_vtm = [[res2("vtm%d_%d" % (p, i)) for i in range(17)] for p in range(2)]
    sall_loc = [r2_f(512) for _ in range(2)]
    sall = [v_f(l).rearrange("p (c v) -> p c v", v=128) for l in sall_loc]
    R_sall = [[res2("sall%d_%d" % (p, j)) for j in range(4)] for p in range(2)]
    sbb_loc = [r2_bf(512) for _ in range(2)]
    sbb = [v_bf(l).rearrange("p (c v) -> p c v", v=128) for l in sbb_loc]
    R_sbb = [[res2("sbb%d_%d" % (p, j)) for j in range(4)] for p in range(2)]
    vm4 = [v_bf(r2_bf(512)) for _ in range(4)]
    R_vm4 = [res2("vm4_%d" % p) for p in range(4)]
    osb = [v_f(r2_f(512)) for _ in range(2)] + [v_f(r2_f(16))]; R_osb = [res2("osb%d" % p) for p in range(3)]
    send = [v_bf(r2_bf(128)) for _ in range(3)]
    R_send = [res2("send%d" % p) for p in range(3)]
    pm4 = [v_bf(r2_bf(512)) for _ in range(2)]
    R_pm4 = [res2("pm4_%d" % p) for p in range(2)]
    pm0 = v_bf(r2_bf(128)); R_pm0 = res2("pm0")
    ktm4 = [v_bf(r2_bf(512)).rearrange("p (g k) -> p g k", k=128) for _ in range(2)]
    R_ktm4 = [res2("ktm4_%d" % p) for p in range(2)]
    ktm0 = v_bf(r2_bf(128)); R_ktm0 = res2("ktm0")
    osq_loc = [r2_bf(512) for _ in range(2)] + [r2_bf(16)]
    osq = [v_bf(l) for l in osq_loc]
    R_osq = [res2("osq%d" % p) for p in range(3)]
    rsd = [v_f(r2_f(512)) for _ in range(2)] + [v_f(r2_f(16))]; R_rsd = [res2("rsd%d" % p) for p in range(3)]
    tno = [v_f(r2_f(512)) for _ in range(2)] + [v_f(r2_f(16))]; R_tno = [res2("tno%d" % p) for p in range(3)]
    ssm_loc = [r2_f(8 * 128) for _ in range(2)]
    ssm = [v_f(l).rearrange("p (n v) -> p n v", v=128) for l in ssm_loc]
    R_ssm = [res2("ssm%d" % p) for p in range(2)]
    ssb_loc = [r2_bf(8 * 128) for _ in range(2)]
    ssb = [v_bf(l).rearrange("p (n v) -> p n v", v=128) for l in ssb_loc]
    R_ssb = [res2("ssb%d" % p) for p in range(2)]
    vm_loc = [r2_bf(8 * 128) for _ in range(2)]
    vmk = [v_bf(l).rearrange("p (n v) -> p n v", v=128) for l in vm_loc]
    R_vm = [res2("vm%d" % p) for p in range(2)]
    ybin = r2[:, 0:8 * T].rearrange("p (c t) -> p c t", t=T)
    R_ybin = [Res("ybin%d" % j) for j in range(8)]

    NSLOT = 14
    r3 = sb("r3", [128, NSLOT * 512], F32)
    r3b = r3[:].bitcast(BF16)
    R_slot = [Res("slot%d" % i) for i in range(NSLOT)]

    def slot_f(i, n=512, nslots=1):
        return r3[:, i * 512:i * 512 + n]

    def slot_bf(i, n):
        return r3b[:, i * 1024:i * 1024 + n]

    NW = 7
    wpool = [sb("wp%d" % i, [128, 8, 128], BF16) for i in range(NW)]
    R_wp = [Res("wp%d" % i) for i in range(NW)]
    wctr = [0]

    def load_w(src2d, c0):
        i = wctr[0] % NW
        wctr[0] += 1
        src = src2d.rearrange("(k p) n -> p k n", p=128)[:, :, c0:c0 + 128]
        S.dma("pool", lambda e, i=i, src=src: e.dma_start(out=wpool[i][:], in_=src), "wp%d" % i,
              writes=[R_wp[i]])
        return wpool[i], R_wp[i]

    ident = sb("ident", [128, 128], BF16)
    identf = sb("identf", [128, 128], F32)
    onesb = sb("onesb", [128, 128], BF16)
    cmask = sb("cmask", [128, 128], F32)
    smask = sb("smask", [32, 16], F32)
    smaskp = sb("smaskp", [32, 16], BF16)
    bmask = sb("bmask", [128, 4], BF16)
    mask0 = sb("mask0", [128, BL], BF16)
    maskn = sb("maskn", [128, BL], BF16)
    lbt = sb("lbt", [128, 2, 8], F32)
    Acol = sb("Acol", [128, 8], F32)
    Bcol = sb("Bcol", [128, 8], F32)
    lnA = sb("lnA", [128, 8], F32)
    cwt = sb("cwt", [128, 3, 8], F32)
    hn = sb("hn", [128, 1], F32)
    epst = sb("epst", [128, 2], F32)
    small = sb("small", [128, 64], F32)
    R_small = [Res("small%d" % i) for i in range(8)]
    egl = [sb("egl%d" % p, [128, NCH], F32) for p in range(2)]
    negl = [sb("negl%d" % p, [128, NCH], F32) for p in range(2)]
    R_egl = [[Res("egl%d_%d" % (p, b)) for b in range(NBLK)] for p in range(2)]
    ctxT = sb("ctxT", [128, 2, 8, NS], F32)
    uc = sb("uc", [128, 8, 18], F32)
    ucT = r3[0:18, 10 * 512:12 * 512]
    sct = r3[0:NS, 10 * 512:14 * 512].rearrange("p (j d) -> p j d", d=D)
    R_c = {n: Res(n) for n in ("ident", "identf", "ones", "cmask", "smask", "smaskp", "masks", "lbt", "AB",
                                 "cwt", "hn", "eps", "small", "ctxT", "uc", "ucT", "sct", "npre", "npost")}

    pbank = [ps("pb%d" % i, [128, 512], F32) for i in range(8)]
    R_pb = [Res("pb%d" % i) for i in range(8)]
    R_pbs = [[Res("pb%d_%d" % (i, j)) for j in range(4)] for i in range(8)]

    def setup():
        S.op("pool", lambda e: e.memset(identf[:], 1.0), writes=[R_c["identf"]])
        S.op("pool", lambda e: e.affine_select(out=identf[:], in_=identf[:], pattern=[[-1, 128]],
                                               compare_op=ALU.is_equal, fill=0.0, base=0, channel_multiplier=1),
             reads=[R_c["identf"]], writes=[R_c["identf"]])
        S.op("pool", lambda e: e.tensor_copy(out=ident[:], in_=identf[:]), reads=[R_c["identf"]],
             writes=[R_c["ident"]])
        S.op("pool", lambda e: e.memset(epst[:, 0:1], EPS), writes=[R_c["eps"]])
        S.op("pool", lambda e: e.memset(epst[:, 1:2], 4.0 * EPS), reads=[R_c["eps"]], writes=[R_c["eps"]])
        S.op("pool", lambda e: e.memset(onesb[:], 1.0), writes=[R_c["ones"]])
        S.op("pool", lambda e: e.memset(cmask[:], -1.0), writes=[R_c["cmask"]])
        S.op("pool", lambda e: e.affine_select(out=cmask[:], in_=cmask[:], pattern=[[1, 128]],
                                               compare_op=ALU.is_ge, fill=0.0, base=0, channel_multiplier=-1),
             reads=[R_c["cmask"]], writes=[R_c["cmask"]])
        for c in range(1, 4):
            S.op("pool", lambda e, c=c: e.affine_select(out=cmask[:, 32 * c:32 * c + 32],
                                                        in_=cmask[:, 32 * c:32 * c + 32], pattern=[[0, 32]],
                                                        compare_op=ALU.is_ge, fill=0.0, base=-32 * c,
                                                        channel_multiplier=1),
                 reads=[R_c["cmask"]], writes=[R_c["cmask"]])
        S.op("pool", lambda e: e.memset(smask[:], -1.0), writes=[R_c["smask"]])
        S.op("pool", lambda e: e.affine_select(out=smask[:], in_=smask[:], pattern=[[-1, 16]],
                                               compare_op=ALU.is_equal, fill=0.0, base=-16, channel_multiplier=1),
             reads=[R_c["smask"]], writes=[R_c["smask"]])
        S.op("pool", lambda e: e.tensor_scalar(out=smaskp[:], in0=smask[:], scalar1=-1.0, scalar2=None,
                                               op0=ALU.mult),
             reads=[R_c["smask"]], writes=[R_c["smaskp"]])
        S.op("pool", lambda e: e.memset(bmask[:], 1.0), writes=[R_c["smaskp"]])
        S.op("pool", lambda e: e.affine_select(out=bmask[:], in_=bmask[:], pattern=[[-32, 4]],
                                               compare_op=ALU.is_ge, fill=0.0, base=0, channel_multiplier=1),
             reads=[R_c["smaskp"]], writes=[R_c["smaskp"]])
        S.op("pool", lambda e: e.affine_select(out=bmask[:], in_=bmask[:], pattern=[[32, 4]],
                                               compare_op=ALU.is_ge, fill=0.0, base=31, channel_multiplier=-1),
             reads=[R_c["smaskp"]], writes=[R_c["smaskp"]])
        S.op("pool", lambda e: e.memset(mask0[:], 1.0), writes=[R_c["masks"]])
        S.op("pool", lambda e: e.memset(maskn[:], 1.0), writes=[R_c["masks"]])
        S.op("pool", lambda e: e.memset(maskn[:].rearrange("p (c t) -> p c t", t=32)[:, :, 0:1], 0.0),
             reads=[R_c["masks"]], writes=[R_c["masks"]])
        S.op("pool", lambda e: e.memset(mask0[:].rearrange("p (c t) -> p c t", t=32)[:, :, 0:1], 0.0),
             reads=[R_c["masks"]], writes=[R_c["masks"]])
        S.op("pool", lambda e: e.memset(mask0[:, 16:32], 0.0), reads=[R_c["masks"]], writes=[R_c["masks"]])
        S.op("pool", lambda e: e.memset(oz[:, :, 0:NMETA], 0.0), writes=R_oz)

    def setup_late():
        S.dma("sp", lambda e: e.dma_start(out=lbt[:], in_=lb_logits.rearrange("r (h p) -> p r h", p=128),
                                          allow_slow_non_contiguous=True),
              "c_lbt", writes=[R_c["lbt"]])
        S.dma("sp", lambda e: e.dma_start(out=cwt[:], in_=conv_w.rearrange("j (c p) -> p j c", p=128),
                                          allow_slow_non_contiguous=True),
              "c_cwt", writes=[R_c["cwt"]])
        S.dma("sp", lambda e: e.dma_start(out=hn[:], in_=hgrn_norm.rearrange("o p -> p o"),
                                          allow_slow_non_contiguous=True),
              "c_hn", writes=[R_c["hn"]])
        S.dma("sp", lambda e: e.dma_start(out=sct, in_=state_conv), "c_sct", writes=R_slot[10:14])
        S.op("dve", lambda e: e.tensor_tensor(out=Acol[:], in0=lbt[:, 0, :], in1=lbt[:, 1, :], op=ALU.subtract),
             reads=[R_c["lbt"]], writes=[R_c["AB"]])
        S.op("act", lambda e: e.activation(out=Bcol[:], in_=Acol[:], func=AF.Tanh, scale=0.5),
             reads=[R_c["AB"]], writes=[R_c["AB"]])
        S.op("dve", lambda e: e.tensor_scalar(out=Acol[:], in0=Bcol[:], scalar1=-0.25, scalar2=0.25,
                                              op0=ALU.mult, op1=ALU.add),
             reads=[R_c["AB"]], writes=[R_c["AB"]])
        S.op("dve", lambda e: e.tensor_scalar(out=Bcol[:], in0=Bcol[:], scalar1=0.25, scalar2=0.75,
                                              op0=ALU.mult, op1=ALU.add),
             reads=[R_c["AB"]], writes=[R_c["AB"]])
        S.op("act", lambda e: e.activation(out=lnA[:], in_=Acol[:], func=AF.Ln),
             reads=[R_c["AB"]], writes=[R_c["AB"]])
        pctx = pbank[7][:, 0:256].rearrange("p (a n) -> p a n", n=NS)
        for j in range(2):
            for ct in range(8):
                S.op("pe", lambda e, j=j, ct=ct: e.transpose(pctx[:, j * 8 + ct, :],
                                                             sct[:, j, ct * 128:(ct + 1) * 128],
                                                             identf[0:NS, 0:NS]),
                     reads=R_slot[10:14] + [R_c["identf"]], writes=[R_pb[7]])
        S.op("dve", lambda e: e.tensor_copy(out=ctxT[:].rearrange("p j c n -> p (j c) n"), in_=pctx),
             reads=[R_pb[7]], writes=[R_c["ctxT"]])
        out_events.append(S.dma("sp", lambda e: e.dma_start(out=nc_sample[:, 0, :], in_=sct[:, 1, :]), "o_ncs0",
                                reads=R_slot[10:14]))

    def tm_tokens(i):
        return (0, 32) if i == 0 else (32 + 128 * (i - 1), 128)

    def blocks_of(t0, n):
        return list(range(t0 // BL, (t0 + n - 1) // BL + 1))

    def phase1():
        npre = slot_f(8, 512), slot_f(9, 512)
        npre_ap = r3[:, 8 * 512:10 * 512]
        S.dma("sp", lambda e: e.dma_start(out=npre_ap, in_=norm_pre.broadcast_to([128, D])), "c_npre",
              writes=[R_slot[8], R_slot[9]])
        pend_copy = []
        for i in range(17):
            t0, P = tm_tokens(i)
            par = i % 2
            sc0 = 8 * (i % 4)
            Rsm = R_small[i % 4]
            x4 = i % 4
            xs = r3[:, (x4 * 2) * 512:(x4 * 2 + 2) * 512]
            Rx = [R_slot[x4 * 2], R_slot[x4 * 2 + 1]]
            xb_ = slot_bf(10 + par, 1024)
            Rxb = [R_slot[10 + par]]
            junk = r3[:, 12 * 512:14 * 512]
            Rj = [R_slot[12], R_slot[13]]
            if i == 0:
                S.dma("sp", lambda e, xs=xs: [e.dma_start(out=xs[0:NMETA, :], in_=meta_tokens),
                                              e.dma_start(out=xs[NMETA:32, :], in_=x_sample)],
                      "xs%d" % x4, writes=Rx, n=2)
            else:
                S.dma("sp" if i % 2 else "pool", lambda e, xs=xs, i=i: e.dma_start(
                    out=xs[:, :], in_=x_prompt[128 * (i - 1):128 * i, :]),
                    "xs%d%s" % (x4, "" if i % 2 else "p"), writes=Rx)
            S.op("act", lambda e, xs=xs, P=P, junk=junk, sc0=sc0: e.activation(out=junk[0:P, :], in_=xs[0:P, :],
                                                                     func=AF.Square,
                                                                     accum_out=small[0:P, sc0:sc0 + 1]),
                 reads=Rx, writes=Rj + [Rsm])
            S.op("act", lambda e, P=P, sc0=sc0: e.activation(out=small[0:P, sc0 + 1:sc0 + 2], in_=small[0:P, sc0:sc0 + 1], func=AF.Ln,
                                                    scale=1.0 / D, bias=epst[0:P, 0:1]),
                 reads=[Rsm, R_c["eps"]], writes=[Rsm])
            S.op("act", lambda e, P=P, sc0=sc0: e.activation(out=small[0:P, sc0 + 2:sc0 + 3], in_=small[0:P, sc0 + 1:sc0 + 2], func=AF.Exp,
                                                    scale=-0.5),
                 reads=[Rsm], writes=[Rsm])
            S.op("dve", lambda e, xs=xs, P=P, xb_=xb_, sc0=sc0: e.scalar_tensor_tensor(
                out=xb_[0:P, :], in0=xs[0:P, :], scalar=small[0:P, sc0 + 2:sc0 + 3], in1=npre_ap[0:P, :],
                op0=ALU.mult, op1=ALU.mult),
                 reads=Rx + [Rsm, R_slot[8], R_slot[9]], writes=Rxb)
            pt = pbank[par][:].bitcast(BF16)[:, 0:8 * P].rearrange("p (k t) -> p k t", t=P)
            for k in range(8):
                S.op("pe", lambda e, k=k, P=P, xb_=xb_, pt=pt: e.transpose(pt[:, k, :],
                                                                          xb_[0:P, k * 128:(k + 1) * 128],
                                                                          ident[0:P, 0:P]),
                     reads=Rxb + [R_c["ident"]], writes=[R_pb[par]])
            if pend_copy:
                pend_copy.pop()()
            pend_copy.append(lambda t0=t0, P=P, pt=pt, par=par: S.op(
                "dve", lambda e: e.tensor_copy(out=xnT[:, :, t0:t0 + P], in_=pt),
                reads=[R_pb[par]], writes=[R_xnT[b] for b in blocks_of(t0, P)]))
        pend_copy.pop()()

    pb_rot = [0]

    def fm_proj(wt, Rw, b, banks):
        bi = banks[pb_rot[0] % len(banks)]
        pb_rot[0] += 1
        pt = pbank[bi][:, 0:BL]
        for k in range(8):
            S.op("pe", lambda e, k=k, pt=pt, wt=wt, b=b: e.matmul(pt, wt[:, k, :], xnT[:, k, b * BL:(b + 1) * BL],
                                                               start=(k == 0), stop=(k == 7)),
                 reads=[Rw, R_xnT[b]], writes=[R_pb[bi]])
        return pt, R_pb[bi]

    th_all = r3[:, 0:T]
    R_th = [Res("th%d" % b) for b in range(NBLK)]
    qa_all = r3b[:, 5 * 1024:5 * 1024 + T]
    R_qa = [Res("qa%d" % b) for b in range(NBLK)]

    A_BANKS_FM = [0, 1, 2]
    B_KT = 3
    B_SC = 3
    B_OT = 4
    B_U = [5, 6]
    B_SSQ = 7

    wcache = {}

    def want_w(key, src2d, col):
        if key not in wcache:
            wcache[key] = load_w(src2d, col)
        return wcache[key]

    B_SEG = {"z": 7, "c": 5, "h": 6, "b": 4}

    def issue_B(j, kinds):
        for kind in kinds:
            want_w(("B", j, kind), w_in, B_SEG[kind] * D + j * 128)

    def issue_M(j, kinds):
        for kind in kinds:
            if kind == "ga":
                want_w(("M", j, kind), w_in, 8 * D + j * 128)
            elif kind == "gb":
                want_w(("M", j, kind), w_in, 9 * D + j * 128)
            elif kind == "wa":
                want_w(("M", j, kind), w_a, j * 128)
            else:
                want_w(("M", j, kind), w_b, j * 128)

    wts = {}

    def issue_w(h, kind):
        if h >= NH or kind in wts.get(h, {}):
            return
        col = {"q": 0, "f": 1, "v": 2, "z": 3}[kind] * D + h * 128
        wts.setdefault(h, {})[kind] = load_w(w_in, col)

    def projA(h):
        par = h % 2
        for kind in ("q", "f", "z", "v"):
            issue_w(h, kind)
        wq, Rwq = wts[h]["q"]
        wf, Rwf = wts[h]["f"]
        wz, Rwz = wts[h]["z"]
        wv, Rwv = wts[h]["v"]
        for b in range(NBLK):
            sl = slice(b * BL, (b + 1) * BL)
            pq, Rpq = fm_proj(wq, Rwq, b, A_BANKS_FM)
            S.op("act", lambda e, pq=pq, sl=sl: e.activation(out=qa_all[:, sl], in_=pq, func=AF.Silu),
                 reads=[Rpq], writes=[R_qa[b]])
            yield
            pf, Rpf = fm_proj(wf, Rwf, b, A_BANKS_FM)
            S.op("act", lambda e, pf=pf, sl=sl: e.activation(out=th_all[:, sl], in_=pf, func=AF.Tanh, scale=0.5),
                 reads=[Rpf], writes=[R_th[b]])
            yield
            pz, Rpz = fm_proj(wz, Rwz, b, A_BANKS_FM)
            S.op("act", lambda e, pz=pz, sl=sl, par=par: e.activation(out=HB[("z", par)][:, sl], in_=pz,
                                                                      func=AF.Silu),
                 reads=[Rpz], writes=[R_HB[("z", par, b)]])
            yield
        for grp in ([0], [1, 2, 3, 4], [5, 6, 7, 8], [9, 10, 11, 12], [13, 14, 15, 16]):
            B_VTM = A_BANKS_FM[pb_rot[0] % len(A_BANKS_FM)]
            pb_rot[0] += 1
            pv = pbank[B_VTM][:].rearrange("p (t v) -> p t v", v=128)
            for qi, i in enumerate(grp):
                t0, P = tm_tokens(i)
                for k in range(8):
                    S.op("pe", lambda e, k=k, qi=qi, t0=t0, P=P, pv=pv: e.matmul(
                        pv[0:P, qi, :], xnT[:, k, t0:t0 + P], wv[:, k, :], start=(k == 0), stop=(k == 7)),
                         reads=[Rwv] + [R_xnT[b] for b in blocks_of(t0, P)], writes=[R_pb[B_VTM]])
            P = 32 if grp == [0] else 128
            n = len(grp)
            S.op("act", lambda e, grp=grp, P=P, n=n, par=par, pv=pv: e.activation(
                out=vtm[par][0:P, grp[0]:grp[0] + n, :], in_=pv[0:P, 0:n, :], func=AF.Copy),
                 reads=[R_pb[B_VTM]], writes=[R_vtm[par][i] for i in grp])
            yield

    def elemA(h, blocks=range(NBLK), kslot=0):
        par = h % 2
        qT, kT, khT = HB[("q", par)], HB[("k", par)], HB[("kh", par)]
        for b in blocks:
            sl = slice(b * BL, (b + 1) * BL)
            lf = slot_f(8 + 2 * kslot, BL)
            G = slot_f(9 + 2 * kslot, BL)
            eG = G
            enG = lf
            Rlf, RG, ReG = R_slot[8 + 2 * kslot], R_slot[9 + 2 * kslot], R_slot[9 + 2 * kslot]
            msk = mask0 if b == 0 else maskn
            S.op("act", lambda e, lf=lf, sl=sl, h=h: e.activation(out=lf, in_=th_all[:, sl], func=AF.Ln,
                                                                  scale=Acol[:, h:h + 1], bias=Bcol[:, h:h + 1]),
                 reads=[R_th[b], R_c["AB"]], writes=[Rlf])
            yield "ln"
            S.op("dve", lambda e, G=G, lf=lf, msk=msk: e.tensor_tensor_scan(out=G, data0=msk[:], data1=lf,
                                                                            initial=0.0, op0=ALU.mult,
                                                                            op1=ALU.add),
                 reads=[Rlf, R_c["masks"]], writes=[RG])
            yield "scan"
            S.op("act", lambda e, enG=enG, G=G, h=h: e.activation(out=enG, in_=G, func=AF.Exp, scale=-1.0,
                                                                  bias=lnA[:, h:h + 1]),
                 reads=[RG, R_c["AB"]], writes=[Rlf])
            S.op("act", lambda e, eG=eG, G=G: e.activation(out=eG, in_=G, func=AF.Exp),
                 reads=[RG], writes=[ReG])
            yield "exp"
            S.op("dve", lambda e, eG=eG, sl=sl, qT=qT: e.tensor_tensor(out=qT[:, sl], in0=qa_all[:, sl], in1=eG,
                                                                       op=ALU.mult),
                 reads=[R_qa[b], ReG], writes=[R_HB[("q", par, b)]])
            S.op("dve", lambda e, enG=enG, sl=sl, kT=kT: e.scalar_tensor_tensor(
                out=kT[:, sl], in0=th_all[:, sl], scalar=1.0, in1=enG, op0=ALU.subtract, op1=ALU.mult),
                 reads=[R_th[b], Rlf], writes=[R_HB[("k", par, b)]])
            if b == 0:
                pieces = [(eG[:, 15:16], 0, 1), (eG[:, 16:32], 1, 16),
                          (eG[:, 32:BL].rearrange("p (c t) -> p c t", t=32)[:, :, 31], 17, 12)]
            else:
                pieces = [(eG.rearrange("p (c t) -> p c t", t=32)[:, :, 31], 17 + 13 * b - 1, 13)]
            for src, c0, n in pieces:
                S.op("dve", lambda e, src=src, c0=c0, n=n, par=par: e.tensor_copy(out=egl[par][:, c0:c0 + n],
                                                                                  in_=src),
                     reads=[ReG], writes=[R_egl[par][b]])
                S.op("dve", lambda e, src=src, c0=c0, n=n, par=par: e.tensor_scalar(
                    out=negl[par][:, c0:c0 + n], in0=src, scalar1=-1.0, scalar2=None, op0=ALU.mult),
                     reads=[ReG], writes=[R_egl[par][b]])
            if b == 0:
                S.op("dve", lambda e, par=par, kT=kT, khT=khT: e.tensor_tensor(
                    out=khT[:, 0:16], in0=kT[:, 0:16], in1=negl[par][:, 0:1].broadcast_to([128, 16]),
                    op=ALU.mult),
                     reads=[R_HB[("k", par, b)], R_egl[par][b]], writes=[R_HB[("kh", par, b)]])
                S.op("dve", lambda e, par=par, kT=kT, khT=khT: e.tensor_tensor(
                    out=khT[:, 16:32], in0=kT[:, 16:32], in1=negl[par][:, 1:17], op=ALU.mult),
                     reads=[R_HB[("k", par, b)], R_egl[par][b]], writes=[R_HB[("kh", par, b)]])
                lo, c0, n = 32, 17, 12
            else:
                lo, c0, n = b * BL, 17 + 13 * b - 1, 13
            S.op("dve", lambda e, par=par, lo=lo, c0=c0, n=n, kT=kT, khT=khT: e.tensor_tensor(
                out=khT[:, lo:lo + 32 * n].rearrange("p (c t) -> p c t", t=32),
                in0=kT[:, lo:lo + 32 * n].rearrange("p (c t) -> p c t", t=32),
                in1=negl[par][:, c0:c0 + n].unsqueeze(2).broadcast_to([128, n, 32]), op=ALU.mult),
                 reads=[R_HB[("k", par, b)], R_egl[par][b]], writes=[R_HB[("kh", par, b)]])
            yield "mul"

    def fillA(h):
        pa = projA(h)
        for gi, grp in enumerate(((0, 1, 2), (3, 4))):
            for b in grp:
                for _ in range(3):
                    next(pa)
                    yield "proj"
            for kind in (("q", "f"), ("z",))[gi]:
                issue_w(h + 1, kind)
            gens = [elemA(h, (b,), k_) for k_, b in enumerate(grp)]
            for stage in range(4):
                tag = None
                for g_ in gens:
                    tag = next(g_)
                    yield ("elem" if (tag == "exp" and g_ is gens[-1]) else "estage")
        for _ in pa:
            yield "v"
        issue_w(h + 1, "v")

    def pump(filler, n):
        for _ in range(n):
            try:
                next(filler)
            except StopIteration:
                return

    misc_rot = [0]

    def misc_slot():
        j = misc_rot[0] % 4
        misc_rot[0] += 1
        return j

    def hb_res(kind, par, t0, n):
        return [R_HB[(kind, par, b)] for b in blocks_of(t0, n)]

    onorm_ctr = [0]

    def onorm_stages(h, par, po, Rpo, t0, n, ssq_bank=None, ssq_col=0, bufidx=0):
        ob_ = bufidx
        ver = {}
        zres = hb_res("z", par, t0, n)
        ver["z"] = [r_.w for r_ in zres]
        osb_, Rosb = osb[ob_], R_osb[ob_]
        osq_, Rosq = osq[ob_], R_osq[ob_]
        rsd_, Rrsd = rsd[ob_], R_rsd[ob_]
        tno_, Rtno = tno[ob_], R_tno[ob_]
        sb_ = B_SSQ if ssq_bank is None else ssq_bank
        pss = pbank[sb_][:, ssq_col:ssq_col + n]

        def s1():
            S.op("act", lambda e: e.activation(out=osb_[:, 0:n], in_=po, func=AF.Copy), reads=Rpo, writes=[Rosb])
            S.op("act", lambda e: e.activation(out=osq_[:, 0:n], in_=osb_[:, 0:n], func=AF.Square),
                 reads=[Rosb], writes=[Rosq])
            ver["osb"], ver["osq"] = Rosb.w, Rosq.w

        def s2():
            assert Rosq.w == ver["osq"], "osq clobbered"
            S.op("pe", lambda e: e.matmul(pss, onesb[:], osq_[:, 0:n], start=True, stop=True),
                 reads=[Rosq, R_c["ones"]], writes=[R_pb[sb_]])
            ver["pss"] = R_pb[sb_].w

        def s3():
            assert R_pb[sb_].w == ver["pss"], "ssq clobbered"
            S.op("act", lambda e: e.activation(out=rsd_[:, 0:n], in_=pss, func=AF.Ln, scale=1.0 / 128,
                                               bias=epst[:, 0:1]),
                 reads=[R_pb[sb_], R_c["eps"]], writes=[Rrsd])
            S.op("act", lambda e: e.activation(out=rsd_[:, 0:n], in_=rsd_[:, 0:n], func=AF.Exp, scale=-0.5),
                 reads=[Rrsd], writes=[Rrsd])
            ver["rsd"] = Rrsd.w

        def s4():
            assert Rosb.w == ver["osb"] and Rrsd.w == ver["rsd"], "osb/rsd clobbered"
            S.op("dve", lambda e: e.scalar_tensor_tensor(out=tno_[:, 0:n], in0=osb_[:, 0:n], scalar=hn[:, 0:1],
                                                         in1=rsd_[:, 0:n], op0=ALU.mult, op1=ALU.mult),
                 reads=[Rosb, Rrsd, R_c["hn"]], writes=[Rtno])
            ver["tno"] = Rtno.w

        def s5():
            assert Rtno.w == ver["tno"], "tno clobbered"
            assert [r_.w for r_ in zres] == ver["z"], "z clobbered"
            S.op("dve", lambda e: e.tensor_tensor(
                out=oz[:, h, t0:t0 + n], in0=tno_[:, 0:n], in1=HB[("z", par)][:, t0:t0 + n], op=ALU.mult),
                 reads=[Rtno] + hb_res("z", par, t0, n), writes=[R_oz[h]])

        return s1, s2, s3, s4, s5

    def hgrn(h, filler=iter(()), carry=None, last=False):
        par = h % 2
        qT, kT, khT = HB[("q", par)], HB[("k", par)], HB[("kh", par)]
        pUb = [pbank[bu][:].rearrange("p (c v) -> p c v", v=128) for bu in B_U]
        sched = {}
        lnexp_pending = []
        if carry is not None:
            for it_c, items in carry[0].items():
                sched.setdefault(it_c, []).extend(items)
            lnexp_pending.extend(carry[1])

        def at(it, fn):
            sched.setdefault(it, []).append(fn)

        def sched_onorm(it, po, Rpo, t0, n, s2_at=None, **kw):
            st_ = onorm_stages(h, par, po, Rpo, t0, n, **kw)
            st_[0]()
            if s2_at is None:
                at(it + 1, ("onorm2", st_))
            else:
                at(s2_at, ("onorm2now", st_))

        def flush_lnexp(it, force):
            keep = []
            for item in lnexp_pending:
                due, st_ = item
                if force or it >= due:
                    st_[2]()
                    at(it + 2, st_[3])
                    at(it + 3, st_[4])
                else:
                    keep.append(item)
            lnexp_pending[:] = keep

        def run_item(it, item):
            if isinstance(item, tuple) and item[0] == "onorm2":
                item[1][1]()
                lnexp_pending.append((it + 1, item[1]))
                return None
            if isinstance(item, tuple) and item[0] == "onorm2now":
                item[1][1]()
                item[1][2]()
                at(it + 2, item[1][3])
                at(it + 3, item[1][4])
                return None
            return item()

        def pumpf(it):
            try:
                tag = next(filler)
            except StopIteration:
                return
            if tag == "elem":
                flush_lnexp(it, True)

        pk0 = pbank[B_KT][:].bitcast(BF16)[0:32, 0:128]
        S.op("pe", lambda e: e.transpose(pk0, khT[:, 0:32], ident[:]),
             reads=hb_res("kh", par, 0, 32) + [R_c["ident"]], writes=[R_pb[B_KT]])
        S.op("dve", lambda e: e.tensor_copy(out=ktm0[0:32, :], in_=pk0),
             reads=[R_pb[B_KT]], writes=[R_ktm0])
        BU1 = B_U[1]
        S.op("pe", lambda e: e.matmul(pUb[1][:, 0, :], ktm0[0:16, :], vtm[par][0:16, 0, :], start=True, stop=True),
             reads=[R_ktm0, R_vtm[par][0]], writes=[R_pb[BU1]])
        S.op("dve", lambda e: e.tensor_copy(out=sall[1][:, 3, :], in_=pUb[1][:, 0, :]),
             reads=[R_pb[BU1]], writes=[R_sall[1][3]])
        S.op("act", lambda e: e.activation(out=send[2][:, :], in_=sall[1][:, 3, :], func=AF.Copy),
             reads=[R_sall[1][3]], writes=[R_send[2]])

        def sample_load():
            for hf in range(2):
                n0 = 8 * hf
                S.dma("sp", lambda e, hf=hf, n0=n0: e.dma_start(
                    out=ssm[hf][:], in_=state_hgrn[n0:n0 + 8, h, :, :].rearrange("n k v -> k n v")),
                      "ssm%d" % hf, writes=[R_ssm[hf]])

        def sample_cast(hf):
            S.op("dve", lambda e: e.tensor_copy(out=ssb[hf][:], in_=ssm[hf][:]),
                 reads=[R_ssm[hf]], writes=[R_ssb[hf]])

        def sample_vmk(hf):
            n0 = 8 * hf
            S.op("pool", lambda e: e.tensor_tensor(
                out=vmk[hf][0:32, :, :], in0=vtm[par][0:32, 0:1, :].broadcast_to([32, 8, 128]),
                in1=smaskp[:, n0:n0 + 8].unsqueeze(2).broadcast_to([32, 8, 128]), op=ALU.mult),
                 reads=[R_vtm[par][0], R_c["smaskp"]], writes=[R_vm[hf]])

        def sample_scores():
            pss_ = pbank[B_SC][0:32, 0:16]
            S.op("pe", lambda e: e.matmul(pss_, kT[:, 0:32], qT[:, 16:32], start=True, stop=True),
                 reads=hb_res("k", par, 0, 32) + hb_res("q", par, 0, 32), writes=[R_pb[B_SC]])
            S.op("dve", lambda e: e.tensor_tensor(out=pm0[0:32, 0:16], in0=pss_, in1=smask[:], op=ALU.mult),
                 reads=[R_pb[B_SC], R_c["smask"]], writes=[R_pm0])

        def sample_round(r, ub):
            hf, q4 = r // 2, r % 2
            n0 = 8 * hf
            bu = B_SC
            S.op("pe", lambda e: e.matmul(
                pbank[bu][:, :], ktm0[0:32, :],
                vmk[hf][0:32, 4 * q4:4 * q4 + 4, :].rearrange("p n v -> p (n v)"), start=True, stop=True),
                 reads=[R_ktm0, R_vm[hf]], writes=[R_pb[bu]])
            for n_ in range(4):
                nn = 4 * q4 + n_
                S.op("dve", lambda e, nn=nn, n_=n_: e.scalar_tensor_tensor(
                    out=ssm[hf][:, nn, :], in0=ssm[hf][:, nn, :], scalar=egl[par][:, 1 + n0 + nn:2 + n0 + nn],
                    in1=pbank[B_SC][:, n_ * 128:(n_ + 1) * 128], op0=ALU.mult, op1=ALU.add),
                     reads=[R_ssm[hf], R_egl[par][0], R_pb[bu]], writes=[R_ssm[hf]])
            if q4 == 1:
                out_events.append(S.dma("sp", lambda e: e.dma_start(
                    out=nh_sample[n0:n0 + 8, h, :, :].rearrange("n k v -> k n v"), in_=ssm[hf][:]),
                    "ssm%d" % hf, reads=[R_ssm[hf]]))

        def sample_oT(it):
            pos = pbank[B_OT][:, 0:16]
            S.op("pe", lambda e: e.matmul(pos, vtm[par][0:32, 0, :], pm0[0:32, 0:16], start=True, stop=False),
                 reads=[R_vtm[par][0], R_pm0], writes=[R_pb[B_OT]])
            for n_ in range(NS):
                hf = n_ // 8
                S.op("pe", lambda e, n_=n_, hf=hf: e.matmul(pos[:, n_:n_ + 1], ssb[hf][:, n_ % 8, :],
                                                          qT[:, 16 + n_:17 + n_], start=False, stop=(n_ == NS - 1)),
                     reads=[R_ssb[hf]] + hb_res("q", par, 0, 32), writes=[R_pb[B_OT]])
            sched_onorm(it, pos, [R_pb[B_OT]], 16, 16, s2_at=13, bufidx=2)

        def quad_kt_pe(quad):
            tq = 32 + 512 * quad
            pk = pbank[B_KT][:].bitcast(BF16)[:, 0:512].rearrange("p (g k) -> p g k", k=128)
            for gq in range(4):
                S.op("pe", lambda e, gq=gq: e.transpose(pk[:, gq, :], khT[:, tq + 128 * gq:tq + 128 * gq + 128],
                                                        ident[:]),
                     reads=hb_res("kh", par, tq + 128 * gq, 128) + [R_c["ident"]], writes=[R_pb[B_KT]])

        def quad_kt_act(quad):
            qp = quad % 2
            pk = pbank[B_KT][:].bitcast(BF16)[:, 0:512].rearrange("p (g k) -> p g k", k=128)
            S.op("dve", lambda e: e.tensor_copy(out=ktm4[qp][:, :, :], in_=pk),
                 reads=[R_pb[B_KT]], writes=[R_ktm4[qp]])

        def quad_sc_pe(quad):
            tq = 32 + 512 * quad
            psc = pbank[B_SC][:, :].rearrange("p (g t) -> p g t", t=128)
            for gq in range(4):
                t0 = tq + 128 * gq
                S.op("pe", lambda e, gq=gq, t0=t0: e.matmul(psc[:, gq, :], kT[:, t0:t0 + 128], qT[:, t0:t0 + 128],
                                                           start=True, stop=True),
                     reads=hb_res("k", par, t0, 128) + hb_res("q", par, t0, 128), writes=[R_pb[B_SC]])

        def quad_mask_dve(quad):
            qp = quad % 2
            psc = pbank[B_SC][:, :].rearrange("p (g t) -> p g t", t=128)
            S.op("dve", lambda e: e.tensor_tensor(
                out=pm4[qp][:, :].rearrange("p (g t) -> p g t", t=128), in0=psc,
                in1=cmask[:].unsqueeze(1).broadcast_to([128, 4, 128]), op=ALU.mult),
                 reads=[R_pb[B_SC], R_c["cmask"]], writes=[R_pm4[qp]])

        def emit_quad(quad):
            quad_kt_pe(quad)
            quad_kt_act(quad)
            quad_sc_pe(quad)
            quad_mask_dve(quad)

        def emit_vm(g):
            vb = g % 4
            S.op("pool", lambda e: e.tensor_tensor(
                out=vm4[vb][:, :].rearrange("p (c v) -> p c v", v=128),
                in0=vtm[par][:, g + 1:g + 2, :].broadcast_to([128, 4, 128]),
                in1=bmask[:].unsqueeze(2).broadcast_to([128, 4, 128]), op=ALU.mult),
                 reads=[R_vtm[par][g + 1], R_c["smaskp"]], writes=[R_vm4[vb]])

        def emit_U(g):
            quad, gq = g // 4, g % 4
            qp, vb, bu = quad % 2, g % 4, B_U[g % 2]
            S.op("pe", lambda e: e.matmul(pbank[bu][:, :], ktm4[qp][:, gq, :], vm4[vb][:, :], start=True, stop=True),
                 reads=[R_ktm4[qp], R_vm4[vb]], writes=[R_pb[bu]])

        def emit_chain(g):
            gp = g % 2
            bu = B_U[g % 2]
            pUg = pUb[g % 2]
            for c in range(4):
                if c == 0:
                    src, Rsrc = sall[1 - gp][:, 3, :], R_sall[1 - gp][3]
                else:
                    src, Rsrc = sall[gp][:, c - 1, :], R_sall[gp][c - 1]
                col = 17 + 4 * g + c
                S.op("dve", lambda e, c=c, src=src, col=col: e.scalar_tensor_tensor(
                    out=sall[gp][:, c, :], in0=src, scalar=egl[par][:, col:col + 1], in1=pUg[:, c, :],
                    op0=ALU.mult, op1=ALU.add),
                     reads=[Rsrc, R_pb[bu]] + [R_egl[par][b] for b in range(NBLK)], writes=[R_sall[gp][c]])
            if DBG_S is not None:
                out_events.append(S.dma("sp", lambda e: e.dma_start(
                    out=DBG_S[h, 4 * g:4 * g + 4, :, :].rearrange("c k v -> k c v"), in_=sall[gp][:, :, :]),
                    "o_dbgs%d" % gp, reads=R_sall[gp]))

        def emit_casts(g):
            gp = g % 2
            S.op("act", lambda e: e.activation(out=sbb[gp][:, 0:3, :], in_=sall[gp][:, 0:3, :], func=AF.Copy),
                 reads=R_sall[gp][0:3], writes=R_sbb[gp][0:3])
            S.op("dve", lambda e: e.tensor_copy(out=send[g % 3][:, :], in_=sall[gp][:, 3, :]),
                 reads=[R_sall[gp][3]], writes=[R_send[g % 3]])

        def emit_oT(g, it):
            gp = g % 2
            t0 = 32 + 128 * g
            quad, gq = g // 4, g % 4
            ob = B_OT
            qp = quad % 2
            po = pbank[ob][:, gq * 128:(gq + 1) * 128]
            S.op("pe", lambda e: e.matmul(po, vtm[par][:, g + 1, :], pm4[qp][:, gq * 128:(gq + 1) * 128],
                                          start=True, stop=False),
                 reads=[R_vtm[par][g + 1], R_pm4[qp]], writes=[R_pb[ob]])
            for c in range(4):
                if c == 0:
                    sbv, Rsb = send[(g - 1) % 3][:, :], R_send[(g - 1) % 3]
                else:
                    sbv, Rsb = sbb[gp][:, c - 1, :], R_sbb[gp][c - 1]
                S.op("pe", lambda e, c=c, sbv=sbv: e.matmul(
                    po[:, 32 * c:32 * c + 32], sbv, qT[:, t0 + 32 * c:t0 + 32 * c + 32], start=False,
                    stop=(c == 3)),
                     reads=[Rsb] + hb_res("q", par, t0, 128), writes=[R_pb[ob]])
            if gq == 3:
                sched_onorm(it, pbank[ob][:, :], [R_pb[ob]], 32 + 512 * quad, 512, bufidx=quad % 2)
                if sample_pending and quad == 1:
                    sample_pending.pop()
                    sample_oT(it)

        sample_pending = [True]
        at(0, sample_load)
        at(2, lambda: sample_cast(0))
        at(3, lambda: sample_vmk(0))
        at(5, lambda: sample_cast(1))
        at(6, lambda: sample_vmk(1))
        at(4, sample_scores)
        for r, it_r in enumerate((12, 13, 14, 15)):
            at(it_r, (lambda r=r: ("round", r)))
        emit_quad(0)
        emit_vm(0)
        emit_vm(1)
        emit_U(0)
        NIT = NGRP + 2
        for it in range(NIT):
            rounds = []
            for item in sched.pop(it, []):
                res = run_item(it, item)
                if isinstance(res, tuple) and res[0] == "round":
                    rounds.append(res[1])
            if it < NGRP:
                g = it
                if g + 2 < NGRP:
                    emit_vm(g + 2)
                if (g + 4) % 4 == 0 and (g + 4) // 4 < 4:
                    quad_kt_pe((g + 4) // 4)
                if (g + 3) % 4 == 0 and (g + 3) // 4 < 4:
                    quad_kt_act((g + 3) // 4)
                if (g + 2) % 4 == 0 and (g + 2) // 4 < 4:
                    quad_sc_pe((g + 2) // 4)
                if (g + 1) % 4 == 0 and (g + 1) // 4 < 4:
                    quad_mask_dve((g + 1) // 4)
                if g + 1 < NGRP:
                    emit_U(g + 1)
                emit_chain(g)
            for r in rounds:
                sample_round(r, it % 2)
            if 0 <= it - 1 < NGRP:
                emit_casts(it - 1)
            pumpf(it)
            if 0 <= it - 2 < NGRP:
                emit_oT(it - 2, it)
            pumpf(it)
            if it % 4 == 1:
                pumpf(it)
            flush_lnexp(it, False)
        out_events.append(S.dma("sp", lambda e: e.dma_start(out=nh_prompt[h, :, :], in_=sall[1][:, 3, :]),
                                "o_nhp", reads=[R_sall[1][3]]))
        for _ in filler:
            pass
        if last:
            it = NIT
            while sched or lnexp_pending:
                for item in sched.pop(it, []):
                    run_item(it, item)
                flush_lnexp(it, False)
                it += 1
                assert it < NIT + 20
            return None
        carry_s = {}
        for it_c, items in sched.items():
            assert it_c >= NIT
            carry_s[it_c - NIT] = items
        carry_l = [(due - NIT, st_) for due, st_ in lnexp_pending]
        return carry_s, carry_l

    B_BANKS = [0, 1, 2, 3, 4, 5, 6, 7]

    def phaseB(j, banks=None):
        banks = B_BANKS if banks is None else banks
        issue_B(j, ("z", "c", "h", "b"))
        wz_, Rwz = wcache[("B", j, "z")]
        wc_, Rwc = wcache[("B", j, "c")]
        wh_, Rwh = wcache[("B", j, "h")]
        wb_, Rwb = wcache[("B", j, "b")]
        for b in range(NBLK):
            sl = slice(b * BL, (b + 1) * BL)
            sp_ = b % 2
            szb, Rszb = slot_f(0 + sp_, BL), R_slot[0 + sp_]
            csb, Rcsb = slot_f(2 + sp_, BL), R_slot[2 + sp_]
            ub, Rub = slot_f(4 + sp_, BL + 2), R_slot[4 + sp_]
            t1, Rt1 = slot_f(6 + sp_, BL), R_slot[6 + sp_]
            t2, Rt2 = slot_f(8 + sp_, BL), R_slot[8 + sp_]
            pz, Rpz = fm_proj(wz_, Rwz, b, banks)
            S.op("act", lambda e, pz=pz, szb=szb: e.activation(out=szb, in_=pz, func=AF.Silu),
                 reads=[Rpz], writes=[Rszb])
            yield "proj"
            pc, Rpc = fm_proj(wc_, Rwc, b, banks)
            S.op("act", lambda e, pc=pc, csb=csb: e.activation(out=csb, in_=pc, func=AF.Copy),
                 reads=[Rpc], writes=[Rcsb])
            yield "proj"
            ph, Rph = fm_proj(wh_, Rwh, b, banks)
            S.op("dve", lambda e, ph=ph, csb=csb, ub=ub: e.tensor_tensor(out=ub[:, 2:2 + BL], in0=csb, in1=ph,
                                                                         op=ALU.mult),
                 reads=[Rcsb, Rph], writes=[Rub])
            yield "proj"
            pb_, Rpb_ = fm_proj(wb_, Rwb, b, banks)
            if b == 0:
                S.op("dve", lambda e, ub=ub, j=j: e.tensor_copy(out=uc[:, j, 0:16], in_=ub[:, 2 + 16:2 + 32]),
                     reads=[Rub], writes=[R_c["uc"]])
                S.op("dve", lambda e, j=j, t1=t1: e.tensor_scalar(out=t1[:, 16:32], in0=ctxT[:, 0, j, :],
                                                                  scalar1=cwt[:, 0, j:j + 1], scalar2=None,
                                                                  op0=ALU.mult),
                     reads=[R_c["ctxT"], R_c["cwt"]], writes=[Rt1])
                S.op("dve", lambda e, j=j, t1=t1: e.scalar_tensor_tensor(
                    out=t1[:, 16:32], in0=ctxT[:, 1, j, :], scalar=cwt[:, 1, j:j + 1], in1=t1[:, 16:32],
                    op0=ALU.mult, op1=ALU.add),
                     reads=[R_c["ctxT"], R_c["cwt"], Rt1], writes=[Rt1])
                S.op("dve", lambda e, j=j, t1=t1, t2=t2, ub=ub: e.scalar_tensor_tensor(
                    out=t2[:, 16:32], in0=ub[:, 2 + 16:2 + 32], scalar=cwt[:, 2, j:j + 1], in1=t1[:, 16:32],
                    op0=ALU.mult, op1=ALU.add),
                     reads=[Rub, R_c["cwt"], Rt1], writes=[Rt2])
                S.op("dve", lambda e, ub=ub: e.tensor_copy(out=ub[:, 2 + 30:2 + 32], in_=ub[:, 2 + 14:2 + 16]),
                     reads=[Rub], writes=[Rub])
                lo = 32
            else:
                prev = slot_f(4 + (1 - sp_), BL + 2)
                S.op("dve", lambda e, ub=ub, prev=prev: e.tensor_copy(out=ub[:, 0:2], in_=prev[:, BL:BL + 2]),
                     reads=[R_slot[4 + (1 - sp_)]], writes=[Rub])
                lo = 0
            n = BL - lo
            S.op("dve", lambda e, ub=ub, t1=t1, lo=lo, n=n, j=j: e.tensor_scalar(
                out=t1[:, lo:lo + n], in0=ub[:, lo:lo + n], scalar1=cwt[:, 0, j:j + 1], scalar2=None, op0=ALU.mult),
                 reads=[Rub, R_c["cwt"]], writes=[Rt1])
            S.op("dve", lambda e, ub=ub, t1=t1, lo=lo, n=n, j=j: e.scalar_tensor_tensor(
                out=t1[:, lo:lo + n], in0=ub[:, lo + 1:lo + 1 + n], scalar=cwt[:, 1, j:j + 1], in1=t1[:, lo:lo + n],
                op0=ALU.mult, op1=ALU.add),
                 reads=[Rub, R_c["cwt"], Rt1], writes=[Rt1])
            S.op("dve", lambda e, ub=ub, t1=t1, t2=t2, lo=lo, n=n, j=j: e.scalar_tensor_tensor(
                out=t2[:, lo:lo + n], in0=ub[:, lo + 2:lo + 2 + n], scalar=cwt[:, 2, j:j + 1], in1=t1[:, lo:lo + n],
                op0=ALU.mult, op1=ALU.add),
                 reads=[Rub, R_c["cwt"], Rt1], writes=[Rt2])
            lo2 = 16 if b == 0 else 0
            n2 = BL - lo2
            S.op("dve", lambda e, pb_=pb_, t2=t2, lo2=lo2, n2=n2: e.tensor_tensor(
                out=t2[:, lo2:lo2 + n2], in0=t2[:, lo2:lo2 + n2], in1=pb_[:, lo2:lo2 + n2], op=ALU.mult),
                 reads=[Rt2, Rpb_], writes=[Rt2])
            S.op("dve", lambda e, t2=t2, szb=szb, lo2=lo2, n2=n2, j=j, b=b: e.tensor_tensor(
                out=ybin[:, j, b * BL + lo2:b * BL + lo2 + n2], in0=t2[:, lo2:lo2 + n2],
                in1=szb[:, lo2:lo2 + n2], op=ALU.mult),
                 reads=[Rt2, Rszb], writes=[R_ybin[j]])
            if b == NBLK - 1:
                S.op("dve", lambda e, ub=ub, j=j: e.tensor_copy(out=uc[:, j, 16:18], in_=ub[:, BL:BL + 2]),
                     reads=[Rub], writes=[R_c["uc"]])
            if b == 1:
                if j + 1 < 8:
                    issue_B(j + 1, ("z", "c", "h"))
                else:
                    issue_M(0, ("ga", "gb", "wa"))
            yield "proj"

    def conv_out():
        for half in range(2):
            pt = pbank[half][0:18, :].rearrange("p (c v) -> p c v", v=128)
            for c4 in range(4):
                ct = half * 4 + c4
                S.op("pe", lambda e, pt=pt, c4=c4, ct=ct: e.transpose(pt[:, c4, :], uc[:, ct, :], identf[:]),
                     reads=[R_c["uc"], R_c["identf"]], writes=[R_pb[half]])
            S.op("dve", lambda e, half=half: e.tensor_copy(out=ucT[:, half * 512:(half + 1) * 512],
                                                           in_=pbank[half][0:18, :]),
                 reads=[R_pb[half]], writes=R_slot[10:12])
        out_events.append(S.dma("sp", lambda e: e.dma_start(out=nc_sample[:, 1, :], in_=ucT[0:16, :]), "o_ncs1",
                                reads=R_slot[10:12]))
        out_events.append(S.dma("sp", lambda e: e.dma_start(out=nc_prompt[:, :], in_=ucT[16:18, :]), "o_ncp",
                                reads=R_slot[10:12]))

    def fm_acc(wt, Rw, src, Rsrc, b, banks):
        bi = banks[pb_rot[0] % len(banks)]
        pb_rot[0] += 1
        pt = pbank[bi][:, 0:BL]
        for c in range(8):
            S.op("pe", lambda e, c=c, pt=pt, wt=wt, b=b, src=src: e.matmul(pt, wt[:, c, :],
                                                                         src[:, c, b * BL:(b + 1) * BL],
                                                                         start=(c == 0), stop=(c == 7)),
                 reads=[Rw, Rsrc[c]], writes=[R_pb[bi]])
        return pt, R_pb[bi]

    WO_OFF = 8 * T
    wo = bass.AP(r2, WO_OFF, [[R2_BYTES // 2, 128], [D, 8], [1, D]])
    R_wo = Res("wo")

    def load_wo():
        R_wo.inherit(*a2_res)
        S.dma("pool", lambda e: [e.dma_start(out=wo[:, c, :],
                                             in_=w_o.rearrange("(k p) n -> p k n", p=128)[:, c, :])
                                 for c in range(8)], "wo", writes=[R_wo], n=8)

    def phaseM(j):
        issue_M(j, ("ga", "gb", "wa", "wb"))
        wga, Rwga = wcache[("M", j, "ga")]
        wgb, Rwgb = wcache[("M", j, "gb")]
        wa_, Rwa = wcache[("M", j, "wa")]
        wb2, Rwb2 = wcache[("M", j, "wb")]
        if j == 0:
            load_wo()
        mbuf = merged_bufs[j]
        for b in range(NBLK):
            sl = slice(b * BL, (b + 1) * BL)
            sp_ = b % 2
            tha, Rtha = slot_f(0 + sp_, BL), R_slot[0 + sp_]
            thb, Rthb = slot_f(2 + sp_, BL), R_slot[2 + sp_]
            m1, Rm1 = slot_f(4 + sp_, BL), R_slot[4 + sp_]
            m2, Rm2 = slot_f(6 + sp_, BL), R_slot[6 + sp_]
            pga, Rpga = fm_proj(wga, Rwga, b, B_BANKS)
            S.op("act", lambda e, pga=pga, tha=tha: e.activation(out=tha, in_=pga, func=AF.Tanh, scale=0.5),
                 reads=[Rpga], writes=[Rtha])
            pgb, Rpgb = fm_proj(wgb, Rwgb, b, B_BANKS)
            S.op("act", lambda e, pgb=pgb, thb=thb: e.activation(out=thb, in_=pgb, func=AF.Tanh, scale=0.5),
                 reads=[Rpgb], writes=[Rthb])
            pya, Rpya = fm_acc(wa_, Rwa, oz, R_oz, b, B_BANKS)
            S.op("dve", lambda e, tha=tha, pya=pya, m1=m1: e.scalar_tensor_tensor(
                out=m1, in0=tha, scalar=1.0, in1=pya, op0=ALU.add, op1=ALU.mult),
                 reads=[Rtha, Rpya], writes=[Rm1])
            pyb, Rpyb = fm_acc(wb2, Rwb2, ybin, R_ybin, b, B_BANKS)
            S.op("dve", lambda e, thb=thb, pyb=pyb, m2=m2: e.scalar_tensor_tensor(
                out=m2, in0=thb, scalar=1.0, in1=pyb, op0=ALU.add, op1=ALU.mult),
                 reads=[Rthb, Rpyb], writes=[Rm2])
            S.op("dve", lambda e, m1=m1, m2=m2, sl=sl, mbuf=mbuf: e.tensor_tensor(out=mbuf[:, sl], in0=m1, in1=m2,
                                                                               op=ALU.add),
                 reads=[Rm1, Rm2], writes=[merged_res[j][b]])
            if b == 1 and j + 1 < 8:
                issue_M(j + 1, ("ga", "gb", "wa"))

    def phaseF():
        npost_ap = r3[:, 12 * 512:14 * 512]
        S.dma("sp", lambda e: e.dma_start(out=npost_ap, in_=norm_post.broadcast_to([128, D])), "c_npost",
              writes=[R_slot[12], R_slot[13]])
        for i in list(range(1, 17)) + [0]:
            t0, P = tm_tokens(i)
            par = i % 2
            xsl = (0, 2, 10)[i % 3]
            xs = r3[:, xsl * 512:(xsl + 2) * 512]
            Rx = [R_slot[xsl], R_slot[xsl + 1]]
            ys = r3[:, (4 + par * 2) * 512:(4 + par * 2 + 2) * 512]
            Ry = [R_slot[4 + par * 2], R_slot[4 + par * 2 + 1]]
            junk = r3[:, 8 * 512:10 * 512]
            Rj = [R_slot[8], R_slot[9]]
            if i == 0:
                S.dma("pool", lambda e, xs=xs: [e.dma_start(out=xs[0:NMETA, :], in_=meta_tokens),
                                                e.dma_start(out=xs[NMETA:32, :], in_=x_sample)],
                      "xf%d" % (i % 3), writes=Rx, n=2)
            else:
                S.dma("pool", lambda e, xs=xs, i=i: e.dma_start(out=xs[:, :], in_=x_prompt[128 * (i - 1):128 * i, :]),
                      "xf%d" % (i % 3), writes=Rx)
            banks = (2 * (i % 4), 2 * (i % 4) + 1)
            sc0 = 8 * (4 + i % 4) - 32 if False else 8 * (i % 4)
            Rsm = R_small[4 + i % 4]
            sc0 = 32 + 8 * (i % 4)
            Rm_all = [merged_res[c][b] for c in range(8) for b in blocks_of(t0, P)]
            for half in range(2):
                po = pbank[banks[half]][0:P, :]
                for c in range(8):
                    S.op("pe", lambda e, po=po, c=c, t0=t0, P=P, half=half: e.matmul(
                        po, merged_bufs[c][:, t0:t0 + P], wo[:, c, half * 512:(half + 1) * 512],
                        start=(c == 0), stop=(c == 7)),
                         reads=[R_wo] + [merged_res[c][b] for b in blocks_of(t0, P)], writes=[R_pb[banks[half]]])
            for half in range(2):
                po = pbank[banks[half]][0:P, :]
                S.op("act", lambda e, po=po, P=P, half=half, junk=junk, sc0=sc0: e.activation(
                    out=junk[0:P, 0:512], in_=po, func=AF.Square, accum_out=small[0:P, sc0 + 4 + half:sc0 + 5 + half]),
                     reads=[R_pb[banks[half]]], writes=Rj + [Rsm])
            S.op("act", lambda e, P=P, sc0=sc0: e.activation(out=small[0:P, sc0 + 6:sc0 + 7],
                                                             in_=small[0:P, sc0 + 4:sc0 + 5], func=AF.Identity,
                                                             bias=small[0:P, sc0 + 5:sc0 + 6]),
                 reads=[Rsm], writes=[Rsm])
            S.op("act", lambda e, P=P, sc0=sc0: e.activation(out=small[0:P, sc0 + 1:sc0 + 2], in_=small[0:P, sc0 + 6:sc0 + 7], func=AF.Ln,
                                                    scale=1.0 / D, bias=epst[0:P, 1:2]),
                 reads=[Rsm, R_c["eps"]], writes=[Rsm])
            S.op("act", lambda e, P=P, sc0=sc0: e.activation(out=small[0:P, sc0 + 2:sc0 + 3], in_=small[0:P, sc0 + 1:sc0 + 2], func=AF.Exp,
                                                    scale=-0.5),
                 reads=[Rsm], writes=[Rsm])
            for half in range(2):
                po = pbank[banks[half]][0:P, :]
                hs = slice(half * 512, (half + 1) * 512)
                S.op("dve", lambda e, po=po, P=P, hs=hs, ys=ys, sc0=sc0: e.scalar_tensor_tensor(
                    out=ys[0:P, hs], in0=po, scalar=small[0:P, sc0 + 2:sc0 + 3], in1=npost_ap[0:P, hs], op0=ALU.mult,
                    op1=ALU.mult),
                     reads=[R_pb[banks[half]], Rsm, R_slot[12], R_slot[13]], writes=Ry)
            S.op("dve", lambda e, ys=ys, xs=xs, P=P: e.tensor_tensor(out=ys[0:P, :], in0=ys[0:P, :],
                                                                     in1=xs[0:P, :], op=ALU.add),
                 reads=Ry + Rx, writes=Ry)
            if i == 0:
                out_events.append(S.dma("sp", lambda e, ys=ys: e.dma_start(out=y_sample[:, :], in_=ys[NMETA:32, :]),
                                        "ys%d" % par, reads=Ry))
            else:
                out_events.append(S.dma("sp", lambda e, ys=ys, i=i: e.dma_start(
                    out=y_prompt[128 * (i - 1):128 * i, :], in_=ys[:, :]), "ys%d" % par, reads=Ry))

    import os as _os
    KSTOP = _os.environ.get("KSTOP", "")

    def record():
        setup()
        if KSTOP == "setup":
            return
        phase1()
        setup_late()
        if KSTOP == "phase1":
            return
        for b in range(NBLK):
            R_th[b].inherit(*R_slot[0:5])
            R_qa[b].inherit(*R_slot[5:8])
        nh_lim = int(KSTOP[1:]) if KSTOP.startswith("A") else NH
        f0 = fillA(0)
        for _ in f0:
            pass
        full = (nh_lim == NH) and not KSTOP.startswith("A")
        b0_started = False
        carry_ = None
        for h in range(nh_lim):
            if h + 1 < nh_lim:
                filler = fillA(h + 1)
            elif full:
                for i in range(8):
                    R_slot[i].inherit(*R_th, *R_qa)
                R_ybin[0].inherit(*R_vtm[0])
                S.op("pool", lambda e: e.memset(ybin[:, 0, 0:NMETA], 0.0), writes=[R_ybin[0]])
                filler = phaseB(0, A_BANKS_FM)
                b0_started = True
            else:
                filler = iter(())
            carry_ = hgrn(h, filler, carry_, last=(h == nh_lim - 1))
        if KSTOP.startswith("A"):
            return
        if not b0_started:
            for i in range(8):
                R_slot[i].inherit(*R_th, *R_qa)
        j0 = 1 if b0_started else 0
        for j in range(j0, 8):
            R_ybin[j].inherit(*a2_res)
        S.op("pool", lambda e: e.memset(ybin[:, j0:8, 0:NMETA], 0.0), writes=R_ybin[j0:8])
        for j in range(j0, 8):
            for _ in phaseB(j):
                pass
        conv_out()
        if KSTOP == "B":
            return
        for j in range(8):
            phaseM(j)
        if KSTOP == "M":
            return
        phaseF()

    record()
    S.emit(st, final_waits=out_events)
    return nc, S


_CACHE = {}


def kernel(x_prompt, x_sample, state_hgrn, state_conv, meta_tokens, w_in, norm_pre, norm_post, lb_logits,
           hgrn_norm, conv_w, w_a, w_b, w_o):
    f = lambda a: np.ascontiguousarray(np.asarray(a, dtype=np.float32))
    x_prompt, x_sample, state_hgrn, state_conv = f(x_prompt), f(x_sample), f(state_hgrn), f(state_conv)
    shared = {
        "meta_tokens": f(meta_tokens), "w_in": f(w_in)[0], "norm_pre": f(norm_pre), "norm_post": f(norm_post),
        "lb_logits": f(lb_logits), "hgrn_norm": f(hgrn_norm), "conv_w": f(conv_w)[0], "w_a": f(w_a)[0],
        "w_b": f(w_b)[0], "w_o": f(w_o)[0],
    }
    in_maps = []
    for c in range(NCORES):
        m = dict(shared)
        m["x_prompt"] = x_prompt[c]
        m["x_sample"] = np.ascontiguousarray(x_sample[NS * c:NS * (c + 1), 0, :])
        m["state_hgrn"] = np.ascontiguousarray(state_hgrn[0, NS * c:NS * (c + 1)])
        m["state_conv"] = np.ascontiguousarray(state_conv[0, NS * c:NS * (c + 1)])
        in_maps.append(m)
    if "nc" not in _CACHE:
        _CACHE["nc"] = build_program()[0]
    nc = _CACHE["nc"]
    res = run_bass_kernel_spmd(nc, in_maps, core_ids=list(range(NCORES)))
    r = res.results
    y_prompt = np.stack([r[c]["y_prompt"] for c in range(NCORES)], 0)
    y_sample = np.concatenate([r[c]["y_sample"] for c in range(NCORES)], 0)[:, None, :]
    nh_p = np.stack([r[c]["nh_prompt"] for c in range(NCORES)], 0)[None]
    nh_s = np.concatenate([r[c]["nh_sample"] for c in range(NCORES)], 0)[None]
    nc_p = np.stack([r[c]["nc_prompt"] for c in range(NCORES)], 0)[None]
    nc_s = np.concatenate([r[c]["nc_sample"] for c in range(NCORES)], 0)[None]
    return (y_prompt.astype(np.float32), y_sample.astype(np.float32), nh_p.astype(np.float32),
            nh_s.astype(np.float32), nc_p.astype(np.float32), nc_s.astype(np.float32))
```

```python
import contextlib
import numpy as np
import concourse.bass as bass
import concourse.mybir as mybir
from concourse.bass_utils import run_bass_kernel_spmd

F32 = mybir.dt.float32
BF16 = mybir.dt.bfloat16
ALU = mybir.AluOpType
AF = mybir.ActivationFunctionType

NCORES = 8
D = 1024
SEQ = 2048
NS = 16
NMETA = 16
T = NMETA + NS + SEQ
NBLK = 5
BL = T // NBLK
NH = 8
EPS = 1e-6
NGRP = 16
NCH = 1 + NS + 64


class Res:
    __slots__ = ("name", "w", "r")

    def __init__(self, name):
        self.name = name
        self.w = None
        self.r = []

    def inherit(self, *olds):
        for o in olds:
            if o.w is not None:
                self.r.append(o.w)
            self.r.extend(o.r)


class Op:
    __slots__ = ("eng", "fn", "deps", "idx", "dma_sem", "dma_val", "signaled", "is_dma")


class Sched:
    ENG = ("pe", "act", "dve", "pool", "sp")

    def __init__(self, nc):
        self.nc = nc
        self.e = {"pe": nc.tensor, "act": nc.scalar, "dve": nc.vector, "pool": nc.gpsimd, "sp": nc.sync}
        self.ops = {k: [] for k in self.ENG}
        self.dma_cnt = {}
        self.dma_sems = {}
        self.eng_sems = {}

    def _collect(self, reads, writes):
        deps = []
        for r in reads:
            if r.w is not None:
                deps.append(r.w)
        for w in writes:
            if w.w is not None:
                deps.append(w.w)
            deps.extend(w.r)
        return deps

    def _finish(self, o, ev, reads, writes):
        self.ops[o.eng].append(o)
        for r in reads:
            r.r.append(ev)
        for w in writes:
            w.w = ev
            w.r = []

    def op(self, eng, fn, reads=(), writes=()):
        o = Op()
        o.eng = eng
        o.fn = fn
        o.deps = self._collect(reads, writes)
        o.idx = len(self.ops[eng])
        o.is_dma = False
        o.signaled = False
        o.dma_sem = None
        self._finish(o, ("eng", eng, o.idx), reads, writes)
        return o

    def dma(self, eng, fn, sem, reads=(), writes=(), n=1):
        o = Op()
        o.eng = eng
        o.fn = fn
        o.deps = self._collect(reads, writes)
        o.idx = len(self.ops[eng])
        o.is_dma = True
        o.signaled = False
        o.dma_sem = sem
        self.dma_cnt[sem] = self.dma_cnt.get(sem, 0) + 16 * n
        o.dma_val = self.dma_cnt[sem]
        ev = ("dma", sem, o.dma_val)
        self._finish(o, ev, reads, writes)
        return ev

    def emit(self, stack, final_waits=()):
        nc = self.nc
        for k in self.ENG:
            for o in self.ops[k]:
                for d in o.deps:
                    if d[0] == "eng":
                        if d[1] == k and k == "pe":
                            continue
                        self.ops[d[1]][d[2]].signaled = True
        val = {}
        for k in self.ENG:
            c = 0
            v = []
            for o in self.ops[k]:
                if o.signaled:
                    c += 1
                v.append(c)
            val[k] = v
        for k in self.ENG:
            self.eng_sems[k] = stack.enter_context(nc.semaphore("s_" + k))
        for s in self.dma_cnt:
            self.dma_sems[s] = stack.enter_context(nc.semaphore("d_" + s))
        for k in self.ENG:
            eng = self.e[k]
            seen = {}
            for o in self.ops[k]:
                need = {}
                for d in o.deps:
                    if d[0] == "eng":
                        if d[1] == k and k == "pe":
                            continue
                        key = ("eng", d[1])
                        v = val[d[1]][d[2]]
                    else:
                        key = ("dma", d[1])
                        v = d[2]
                    if seen.get(key, 0) >= v:
                        continue
                    if need.get(key, 0) < v:
                        need[key] = v
                for key, v in need.items():
                    sem = self.eng_sems[key[1]] if key[0] == "eng" else self.dma_sems[key[1]]
                    eng.wait_ge(sem, v)
                    seen[key] = v
                ins = o.fn(eng)
                if o.is_dma:
                    if not isinstance(ins, (list, tuple)):
                        ins = [ins]
                    for i_ in ins:
                        i_.then_inc(self.dma_sems[o.dma_sem], 16)
                elif o.signaled:
                    ins.then_inc(self.eng_sems[k], 1)
        for d in final_waits:
            nc.sync.wait_ge(self.dma_sems[d[1]], d[2])


def build_program():
    nc = bass.Bass("TRN2", target_bir_lowering=False)
    S = Sched(nc)
    st = contextlib.ExitStack()

    def din(name, shape):
        return nc.dram_tensor(name, shape, F32, kind="ExternalInput").ap()

    def dout(name, shape):
        return nc.dram_tensor(name, shape, F32, kind="ExternalOutput").ap()

    x_prompt = din("x_prompt", [SEQ, D])
    x_sample = din("x_sample", [NS, D])
    state_hgrn = din("state_hgrn", [NS, NH, 128, 128])
    state_conv = din("state_conv", [NS, 2, D])
    meta_tokens = din("meta_tokens", [NMETA, D])
    w_in = din("w_in", [D, 10 * D])
    norm_pre = din("norm_pre", [1, D])
    norm_post = din("norm_post", [1, D])
    lb_logits = din("lb_logits", [2, D])
    hgrn_norm = din("hgrn_norm", [1, 128])
    conv_w = din("conv_w", [3, D])
    w_a = din("w_a", [D, D])
    w_b = din("w_b", [D, D])
    w_o = din("w_o", [D, D])

    y_prompt = dout("y_prompt", [SEQ, D])
    y_sample = dout("y_sample", [NS, D])
    nh_prompt = dout("nh_prompt", [NH, 128, 128])
    nh_sample = dout("nh_sample", [NS, NH, 128, 128])
    nc_prompt = dout("nc_prompt", [2, D])
    nc_sample = dout("nc_sample", [NS, 2, D])

    def sb(name, shape, dt):
        return st.enter_context(nc.sbuf_tensor(name, shape, dt))

    import os as _os0
    DBG_S = nc.dram_tensor("dbg_S", [NH, 64, 128, 128], F32, kind="ExternalOutput").ap() \
        if _os0.environ.get("KDEBUG") else None

    def ps(name, shape, dt):
        return st.enter_context(nc.psum_tensor(name, shape, dt))

    out_events = []

    xnT = sb("xnT", [128, 8, T], BF16)
    oz = sb("oz", [128, 8, T], BF16)
    R_xnT = [Res("xnT%d" % b) for b in range(NBLK)]
    R_xnT_tile = [Res("xnTt%d" % i) for i in range(17)]
    R_oz = [Res("oz%d" % h) for h in range(NH)]
    HB = {}
    R_HB = {}
    for kind in ("q", "k", "kh", "z"):
        for par in range(2):
            HB[(kind, par)] = sb("hb_%s%d" % (kind, par), [128, T], BF16)
            for b in range(NBLK):
                R_HB[(kind, par, b)] = Res("hb_%s%d_%d" % (kind, par, b))
    merged_bufs = [HB[(kind, par)] for par in range(2) for kind in ("q", "k", "kh", "z")]
    merged_res = [[R_HB[(kind, par, b)] for b in range(NBLK)] for par in range(2) for kind in ("q", "k", "kh", "z")]

    R2_BYTES = 55 * 1024
    r2 = sb("r2", [128, R2_BYTES // 2], BF16)
    r2f = r2[:].bitcast(F32)
    r2_off = [0]

    def r2_bf(nelem, shape=None):
        o = r2_off[0]
        assert o % 4 == 0
        r2_off[0] += ((nelem * 2 + 31) // 32) * 32
        assert r2_off[0] <= R2_BYTES, r2_off[0]
        return (o // 2, nelem)

    def r2_f(nelem):
        o = r2_off[0]
        r2_off[0] += ((nelem * 4 + 31) // 32) * 32
        assert r2_off[0] <= R2_BYTES, r2_off[0]
        return (o // 4, nelem)

    def v_bf(loc):
        return r2[:, loc[0]:loc[0] + loc[1]]

    def v_f(loc):
        return r2f[:, loc[0]:loc[0] + loc[1]]

    a2_res = []

    def res2(name):
        r = Res(name)
        a2_res.append(r)
        return r

    vtm_loc = [r2_bf(17 * 128) for _ in range(2)]
    vtm = [v_bf(l).rearrange("p (t v) -> p t v", v=128) for l in vtm_loc]
    R_vtm = [[res2("vtm%d_%d" % (p, i)) for i in range(17)] for p in range(2)]
    sall_loc = [r2_f(512) for _ in range(2)]
    sall = [v_f(l).rearrange("p (c v) -> p c v", v=128) for l in sall_loc]
    R_sall = [[res2("sall%d_%d" % (p, j)) for j in range(4)] for p in range(2)]
    sbb_loc = [r2_bf(512) for _ in range(2)]
    sbb = [v_bf(l).rearrange("p (c v) -> p c v", v=128) for l in sbb_loc]
    R_sbb = [[res2("sbb%d_%d" % (p, j)) for j in range(4)] for p in range(2)]
    vm4 = [v_bf(r2_bf(512)) for _ in range(4)]
    R_vm4 = [res2("vm4_%d" % p) for p in range(4)]
    osb = [v_f(r2_f(512)) for _ in range(2)] + [v_f(r2_f(16))]; R_osb = [res2("osb%d" % p) for p in range(3)]
    send = [v_bf(r2_bf(128)) for _ in range(3)]
    R_send = [res2("send%d" % p) for p in range(3)]
    pm4 = [v_bf(r2_bf(512)) for _ in range(2)]
    R_pm4 = [res2("pm4_%d" % p) for p in range(2)]
    pm0 = v_bf(r2_bf(128)); R_pm0 = res2("pm0")
    ktm4 = [v_bf(r2_bf(512)).rearrange("p (g k) -> p g k", k=128) for _ in range(2)]
    R_ktm4 = [res2("ktm4_%d" % p) for p in range(2)]
    ktm0 = v_bf(r2_bf(128)); R_ktm0 = res2("ktm0")
    osq_loc = [r2_bf(512) for _ in range(2)] + [r2_bf(16)]
    osq = [v_bf(l) for l in osq_loc]
    R_osq = [res2("osq%d" % p) for p in range(3)]
    rsd = [v_f(r2_f(512)) for _ in range(2)] + [v_f(r2_f(16))]; R_rsd = [res2("rsd%d" % p) for p in range(3)]
    tno = [v_f(r2_f(512)) for _ in range(2)] + [v_f(r2_f(16))]; R_tno = [res2("tno%d" % p) for p in range(3)]
    ssm_loc = [r2_f(8 * 128) for _ in range(2)]
    ssm = [v_f(l).rearrange("p (n v) -> p n v", v=128) for l in ssm_loc]
    R_ssm = [res2("ssm%d" % p) for p in range(2)]
    ssb_loc = [r2_bf(8 * 128) for _ in range(2)]
    ssb = [v_bf(l).rearrange("p (n v) -> p n v", v=128) for l in ssb_loc]
    R_ssb = [res2("ssb%d" % p) for p in range(2)]
    vm_loc = [r2_bf(8 * 128) for _ in range(2)]
    vmk = [v_bf(l).rearrange("p (n v) -> p n v", v=128) for l in vm_loc]
    R_vm = [res2("vm%d" % p) for p in range(2)]
    ybin = r2[:, 0:8 * T].rearrange("p (c t) -> p c t", t=T)
    R_ybin = [Res("ybin%d" % j) for j in range(8)]

    NSLOT = 14
    r3 = sb("r3", [128, NSLOT * 512], F32)
    r3b = r3[:].bitcast(BF16)
    R_slot = [Res("slot%d" % i) for i in range(NSLOT)]

    def slot_f(i, n=512, nslots=1):
        return r3[:, i * 512:i * 512 + n]

    def slot_bf(i, n):
        return r3b[:, i * 1024:i * 1024 + n]

    NW = 7
    wpool = [sb("wp%d" % i, [128, 8, 128], BF16) for i in range(NW)]
    R_wp = [Res("wp%d" % i) for i in range(NW)]
    wctr = [0]

    def load_w(src2d, c0):
        i = wctr[0] % NW
        wctr[0] += 1
        src = src2d.rearrange("(k p) n -> p k n", p=128)[:, :, c0:c0 + 128]
        S.dma("pool", lambda e, i=i, src=src: e.dma_start(out=wpool[i][:], in_=src), "wp%d" % i,
              writes=[R_wp[i]])
        return wpool[i], R_wp[i]

    ident = sb("ident", [128, 128], BF16)
    identf = sb("identf", [128, 128], F32)
    onesb = sb("onesb", [128, 128], BF16)
    cmask = sb("cmask", [128, 128], F32)
    smask = sb("smask", [32, 16], F32)
    smaskp = sb("smaskp", [32, 16], BF16)
    bmask = sb("bmask", [128, 4], BF16)
    mask0 = sb("mask0", [128, BL], BF16)
    maskn = sb("maskn", [128, BL], BF16)
    lbt = sb("lbt", [128, 2, 8], F32)
    Acol = sb("Acol", [128, 8], F32)
    Bcol = sb("Bcol", [128, 8], F32)
    lnA = sb("lnA", [128, 8], F32)
    cwt = sb("cwt", [128, 3, 8], F32)
    hn = sb("hn", [128, 1], F32)
    epst = sb("epst", [128, 2], F32)
    small = sb("small", [128, 64], F32)
    R_small = [Res("small%d" % i) for i in range(8)]
    egl = [sb("egl%d" % p, [128, NCH], F32) for p in range(2)]
    negl = [sb("negl%d" % p, [128, NCH], F32) for p in range(2)]
    R_egl = [[Res("egl%d_%d" % (p, b)) for b in range(NBLK)] for p in range(2)]
    ctxT = sb("ctxT", [128, 2, 8, NS], F32)
    uc = sb("uc", [128, 8, 18], F32)
    ucT = r3[0:18, 10 * 512:12 * 512]
    sct = r3[0:NS, 10 * 512:14 * 512].rearrange("p (j d) -> p j d", d=D)
    R_c = {n: Res(n) for n in ("ident", "identf", "ones", "cmask", "smask", "smaskp", "masks", "lbt", "AB",
                                 "cwt", "hn", "eps", "small", "ctxT", "uc", "ucT", "sct", "npre", "npost")}

    pbank = [ps("pb%d" % i, [128, 512], F32) for i in range(8)]
    R_pb = [Res("pb%d" % i) for i in range(8)]
    R_pbs = [[Res("pb%d_%d" % (i, j)) for j in range(4)] for i in range(8)]

    def setup():
        S.op("pool", lambda e: e.memset(identf[:], 1.0), writes=[R_c["identf"]])
        S.op("pool", lambda e: e.affine_select(out=identf[:], in_=identf[:], pattern=[[-1, 128]],
                                               compare_op=ALU.is_equal, fill=0.0, base=0, channel_multiplier=1),
             reads=[R_c["identf"]], writes=[R_c["identf"]])
        S.op("pool", lambda e: e.tensor_copy(out=ident[:], in_=identf[:]), reads=[R_c["identf"]],
             writes=[R_c["ident"]])
        S.op("pool", lambda e: e.memset(epst[:, 0:1], EPS), writes=[R_c["eps"]])
        S.op("pool", lambda e: e.memset(epst[:, 1:2], 4.0 * EPS), reads=[R_c["eps"]], writes=[R_c["eps"]])
        S.op("pool", lambda e: e.memset(onesb[:], 1.0), writes=[R_c["ones"]])
        S.op("pool", lambda e: e.memset(cmask[:], -1.0), writes=[R_c["cmask"]])
        S.op("pool", lambda e: e.affine_select(out=cmask[:], in_=cmask[:], pattern=[[1, 128]],
                                               compare_op=ALU.is_ge, fill=0.0, base=0, channel_multiplier=-1),
             reads=[R_c["cmask"]], writes=[R_c["cmask"]])
        for c in range(1, 4):
            S.op("pool", lambda e, c=c: e.affine_select(out=cmask[:, 32 * c:32 * c + 32],
                                                        in_=cmask[:, 32 * c:32 * c + 32], pattern=[[0, 32]],
                                                        compare_op=ALU.is_ge, fill=0.0, base=-32 * c,
                                                        channel_multiplier=1),
                 reads=[R_c["cmask"]], writes=[R_c["cmask"]])
        S.op("pool", lambda e: e.memset(smask[:], -1.0), writes=[R_c["smask"]])
        S.op("pool", lambda e: e.affine_select(out=smask[:], in_=smask[:], pattern=[[-1, 16]],
                                               compare_op=ALU.is_equal, fill=0.0, base=-16, channel_multiplier=1),
             reads=[R_c["smask"]], writes=[R_c["smask"]])
        S.op("pool", lambda e: e.tensor_scalar(out=smaskp[:], in0=smask[:], scalar1=-1.0, scalar2=None,
                                               op0=ALU.mult),
             reads=[R_c["smask"]], writes=[R_c["smaskp"]])
        S.op("pool", lambda e: e.memset(bmask[:], 1.0), writes=[R_c["smaskp"]])
        S.op("pool", lambda e: e.affine_select(out=bmask[:], in_=bmask[:], pattern=[[-32, 4]],
                                               compare_op=ALU.is_ge, fill=0.0, base=0, channel_multiplier=1),
             reads=[R_c["smaskp"]], writes=[R_c["smaskp"]])
        S.op("pool", lambda e: e.affine_select(out=bmask[:], in_=bmask[:], pattern=[[32, 4]],
                                               compare_op=ALU.is_ge, fill=0.0, base=31, channel_multiplier=-1),
             reads=[R_c["smaskp"]], writes=[R_c["smaskp"]])
        S.op("pool", lambda e: e.memset(mask0[:], 1.0), writes=[R_c["masks"]])
        S.op("pool", lambda e: e.memset(maskn[:], 1.0), writes=[R_c["masks"]])
        S.op("pool", lambda e: e.memset(maskn[:].rearrange("p (c t) -> p c t", t=32)[:, :, 0:1], 0.0),
             reads=[R_c["masks"]], writes=[R_c["masks"]])
        S.op("pool", lambda e: e.memset(mask0[:].rearrange("p (c t) -> p c t", t=32)[:, :, 0:1], 0.0),
             reads=[R_c["masks"]], writes=[R_c["masks"]])
        S.op("pool", lambda e: e.memset(mask0[:, 16:32], 0.0), reads=[R_c["masks"]], writes=[R_c["masks"]])
        S.op("pool", lambda e: e.memset(oz[:, :, 0:NMETA], 0.0), writes=R_oz)

    def setup_late():
        S.dma("sp", lambda e: e.dma_start(out=lbt[:], in_=lb_logits.rearrange("r (h p) -> p r h", p=128),
                                          allow_slow_non_contiguous=True),
              "c_lbt", writes=[R_c["lbt"]])
        S.dma("sp", lambda e: e.dma_start(out=cwt[:], in_=conv_w.rearrange("j (c p) -> p j c", p=128),
                                          allow_slow_non_contiguous=True),
              "c_cwt", writes=[R_c["cwt"]])
        S.dma("sp", lambda e: e.dma_start(out=hn[:], in_=hgrn_norm.rearrange("o p -> p o"),
                                          allow_slow_non_contiguous=True),
              "c_hn", writes=[R_c["hn"]])
        S.dma("sp", lambda e: e.dma_start(out=sct, in_=state_conv), "c_sct", writes=R_slot[10:14])
        S.op("dve", lambda e: e.tensor_tensor(out=Acol[:], in0=lbt[:, 0, :], in1=lbt[:, 1, :], op=ALU.subtract),
             reads=[R_c["lbt"]], writes=[R_c["AB"]])
        S.op("act", lambda e: e.activation(out=Bcol[:], in_=Acol[:], func=AF.Tanh, scale=0.5),
             reads=[R_c["AB"]], writes=[R_c["AB"]])
        S.op("dve", lambda e: e.tensor_scalar(out=Acol[:], in0=Bcol[:], scalar1=-0.25, scalar2=0.25,
                                              op0=ALU.mult, op1=ALU.add),
             reads=[R_c["AB"]], writes=[R_c["AB"]])
        S.op("dve", lambda e: e.tensor_scalar(out=Bcol[:], in0=Bcol[:], scalar1=0.25, scalar2=0.75,
                                              op0=ALU.mult, op1=ALU.add),
             reads=[R_c["AB"]], writes=[R_c["AB"]])
        S.op("act", lambda e: e.activation(out=lnA[:], in_=Acol[:], func=AF.Ln),
             reads=[R_c["AB"]], writes=[R_c["AB"]])
        pctx = pbank[7][:, 0:256].rearrange("p (a n) -> p a n", n=NS)
        for j in range(2):
            for ct in range(8):
                S.op("pe", lambda e, j=j, ct=ct: e.transpose(pctx[:, j * 8 + ct, :],
                                                             sct[:, j, ct * 128:(ct + 1) * 128],
                                                             identf[0:NS, 0:NS]),
                     reads=R_slot[10:14] + [R_c["identf"]], writes=[R_pb[7]])
        S.op("dve", lambda e: e.tensor_copy(out=ctxT[:].rearrange("p j c n -> p (j c) n"), in_=pctx),
             reads=[R_pb[7]], writes=[R_c["ctxT"]])
        out_events.append(S.dma("sp", lambda e: e.dma_start(out=nc_sample[:, 0, :], in_=sct[:, 1, :]), "o_ncs0",
                                reads=R_slot[10:14]))

    def tm_tokens(i):
        return (0, 32) if i == 0 else (32 + 128 * (i - 1), 128)

    def blocks_of(t0, n):
        return list(range(t0 // BL, (t0 + n - 1) // BL + 1))

    def phase1():
        npre = slot_f(8, 512), slot_f(9, 512)
        npre_ap = r3[:, 8 * 512:10 * 512]
        S.dma("sp", lambda e: e.dma_start(out=npre_ap, in_=norm_pre.broadcast_to([128, D])), "c_npre",
              writes=[R_slot[8], R_slot[9]])
        pend_copy = []
        for i in range(17):
            t0, P = tm_tokens(i)
            par = i % 2
            sc0 = 8 * (i % 4)
            Rsm = R_small[i % 4]
            x4 = i % 4
            xs = r3[:, (x4 * 2) * 512:(x4 * 2 + 2) * 512]
            Rx = [R_slot[x4 * 2], R_slot[x4 * 2 + 1]]
            xb_ = slot_bf(10 + par, 1024)
            Rxb = [R_slot[10 + par]]
            junk = r3[:, 12 * 512:14 * 512]
            Rj = [R_slot[12], R_slot[13]]
            if i == 0:
                S.dma("sp", lambda e, xs=xs: [e.dma_start(out=xs[0:NMETA, :], in_=meta_tokens),
                                              e.dma_start(out=xs[NMETA:32, :], in_=x_sample)],
                      "xs%d" % x4, writes=Rx, n=2)
            else:
                S.dma("sp" if i % 2 else "pool", lambda e, xs=xs, i=i: e.dma_start(
                    out=xs[:, :], in_=x_prompt[128 * (i - 1):128 * i, :]),
                    "xs%d%s" % (x4, "" if i % 2 else "p"), writes=Rx)
            S.op("act", lambda e, xs=xs, P=P, junk=junk, sc0=sc0: e.activation(out=junk[0:P, :], in_=xs[0:P, :],
                                                                     func=AF.Square,
                                                                     accum_out=small[0:P, sc0:sc0 + 1]),
                 reads=Rx, writes=Rj + [Rsm])
            S.op("act", lambda e, P=P, sc0=sc0: e.activation(out=small[0:P, sc0 + 1:sc0 + 2], in_=small[0:P, sc0:sc0 + 1], func=AF.Ln,
                                                    scale=1.0 / D, bias=epst[0:P, 0:1]),
                 reads=[Rsm, R_c["eps"]], writes=[Rsm])
            S.op("act", lambda e, P=P, sc0=sc0: e.activation(out=small[0:P, sc0 + 2:sc0 + 3], in_=small[0:P, sc0 + 1:sc0 + 2], func=AF.Exp,
                                                    scale=-0.5),
                 reads=[Rsm], writes=[Rsm])
            S.op("dve", lambda e, xs=xs, P=P, xb_=xb_, sc0=sc0: e.scalar_tensor_tensor(
                out=xb_[0:P, :], in0=xs[0:P, :], scalar=small[0:P, sc0 + 2:sc0 + 3], in1=npre_ap[0:P, :],
                op0=ALU.mult, op1=ALU.mult),
                 reads=Rx + [Rsm, R_slot[8], R_slot[9]], writes=Rxb)
            pt = pbank[par][:].bitcast(BF16)[:, 0:8 * P].rearrange("p (k t) -> p k t", t=P)
            for k in range(8):
                S.op("pe", lambda e, k=k, P=P, xb_=xb_, pt=pt: e.transpose(pt[:, k, :],
                                                                          xb_[0:P, k * 128:(k + 1) * 128],
                                                                          ident[0:P, 0:P]),
                     reads=Rxb + [R_c["ident"]], writes=[R_pb[par]])
            if pend_copy:
                pend_copy.pop()()
            pend_copy.append(lambda t0=t0, P=P, pt=pt, par=par: S.op(
                "dve", lambda e: e.tensor_copy(out=xnT[:, :, t0:t0 + P], in_=pt),
                reads=[R_pb[par]], writes=[R_xnT[b] for b in blocks_of(t0, P)]))
        pend_copy.pop()()

    pb_rot = [0]

    def fm_proj(wt, Rw, b, banks):
        bi = banks[pb_rot[0] % len(banks)]
        pb_rot[0] += 1
        pt = pbank[bi][:, 0:BL]
        for k in range(8):
            S.op("pe", lambda e, k=k, pt=pt, wt=wt, b=b: e.matmul(pt, wt[:, k, :], xnT[:, k, b * BL:(b + 1) * BL],
                                                               start=(k == 0), stop=(k == 7)),
                 reads=[Rw, R_xnT[b]], writes=[R_pb[bi]])
        return pt, R_pb[bi]

    th_all = r3[:, 0:T]
    R_th = [Res("th%d" % b) for b in range(NBLK)]
    qa_all = r3b[:, 5 * 1024:5 * 1024 + T]
    R_qa = [Res("qa%d" % b) for b in range(NBLK)]

    A_BANKS_FM = [0, 1, 2]
    B_KT = 3
    B_SC = 3
    B_OT = 4
    B_U = [5, 6]
    B_SSQ = 7

    wcache = {}

    def want_w(key, src2d, col):
        if key not in wcache:
            wcache[key] = load_w(src2d, col)
        return wcache[key]

    B_SEG = {"z": 7, "c": 5, "h": 6, "b": 4}

    def issue_B(j, kinds):
        for kind in kinds:
            want_w(("B", j, kind), w_in, B_SEG[kind] * D + j * 128)

    def issue_M(j, kinds):
        for kind in kinds:
            if kind == "ga":
                want_w(("M", j, kind), w_in, 8 * D + j * 128)
            elif kind == "gb":
                want_w(("M", j, kind), w_in, 9 * D + j * 128)
            elif kind == "wa":
                want_w(("M", j, kind), w_a, j * 128)
            else:
                want_w(("M", j, kind), w_b, j * 128)

    wts = {}

    def issue_w(h, kind):
        if h >= NH or kind in wts.get(h, {}):
            return
        col = {"q": 0, "f": 1, "v": 2, "z": 3}[kind] * D + h * 128
        wts.setdefault(h, {})[kind] = load_w(w_in, col)

    def projA(h):
        par = h % 2
        for kind in ("q", "f", "z", "v"):
            issue_w(h, kind)
        wq, Rwq = wts[h]["q"]
        wf, Rwf = wts[h]["f"]
        wz, Rwz = wts[h]["z"]
        wv, Rwv = wts[h]["v"]
        for b in range(NBLK):
            sl = slice(b * BL, (b + 1) * BL)
            pq, Rpq = fm_proj(wq, Rwq, b, A_BANKS_FM)
            S.op("act", lambda e, pq=pq, sl=sl: e.activation(out=qa_all[:, sl], in_=pq, func=AF.Silu),
                 reads=[Rpq], writes=[R_qa[b]])
            yield
            pf, Rpf = fm_proj(wf, Rwf, b, A_BANKS_FM)
            S.op("act", lambda e, pf=pf, sl=sl: e.activation(out=th_all[:, sl], in_=pf, func=AF.Tanh, scale=0.5),
                 reads=[Rpf], writes=[R_th[b]])
            yield
            pz, Rpz = fm_proj(wz, Rwz, b, A_BANKS_FM)
            S.op("act", lambda e, pz=pz, sl=sl, par=par: e.activation(out=HB[("z", par)][:, sl], in_=pz,
                                                                      func=AF.Silu),
                 reads=[Rpz], writes=[R_HB[("z", par, b)]])
            yield
        for grp in ([0], [1, 2, 3, 4], [5, 6, 7, 8], [9, 10, 11, 12], [13, 14, 15, 16]):
            B_VTM = A_BANKS_FM[pb_rot[0] % len(A_BANKS_FM)]
            pb_rot[0] += 1
            pv = pbank[B_VTM][:].rearrange("p (t v) -> p t v", v=128)
            for qi, i in enumerate(grp):
                t0, P = tm_tokens(i)
                for k in range(8):
                    S.op("pe", lambda e, k=k, qi=qi, t0=t0, P=P, pv=pv: e.matmul(
                        pv[0:P, qi, :], xnT[:, k, t0:t0 + P], wv[:, k, :], start=(k == 0), stop=(k == 7)),
                         reads=[Rwv] + [R_xnT[b] for b in blocks_of(t0, P)], writes=[R_pb[B_VTM]])
            P = 32 if grp == [0] else 128
            n = len(grp)
            S.op("act", lambda e, grp=grp, P=P, n=n, par=par, pv=pv: e.activation(
                out=vtm[par][0:P, grp[0]:grp[0] + n, :], in_=pv[0:P, 0:n, :], func=AF.Copy),
                 reads=[R_pb[B_VTM]], writes=[R_vtm[par][i] for i in grp])
            yield

    def elemA(h, blocks=range(NBLK), kslot=0):
        par = h % 2
        qT, kT, khT = HB[("q", par)], HB[("k", par)], HB[("kh", par)]
        for b in blocks:
            sl = slice(b * BL, (b + 1) * BL)
            lf = slot_f(8 + 2 * kslot, BL)
            G = slot_f(9 + 2 * kslot, BL)
            eG = G
            enG = lf
            Rlf, RG, ReG = R_slot[8 + 2 * kslot], R_slot[9 + 2 * kslot], R_slot[9 + 2 * kslot]
            msk = mask0 if b == 0 else maskn
            S.op("act", lambda e, lf=lf, sl=sl, h=h: e.activation(out=lf, in_=th_all[:, sl], func=AF.Ln,
                                                                  scale=Acol[:, h:h + 1], bias=Bcol[:, h:h + 1]),
                 reads=[R_th[b], R_c["AB"]], writes=[Rlf])
            yield "ln"
            S.op("dve", lambda e, G=G, lf=lf, msk=msk: e.tensor_tensor_scan(out=G, data0=msk[:], data1=lf,
                                                                            initial=0.0, op0=ALU.mult,
                                                                            op1=ALU.add),
                 reads=[Rlf, R_c["masks"]], writes=[RG])
            yield "scan"
            S.op("act", lambda e, enG=enG, G=G, h=h: e.activation(out=enG, in_=G, func=AF.Exp, scale=-1.0,
                                                                  bias=lnA[:, h:h + 1]),
                 reads=[RG, R_c["AB"]], writes=[Rlf])
            S.op("act", lambda e, eG=eG, G=G: e.activation(out=eG, in_=G, func=AF.Exp),
                 reads=[RG], writes=[ReG])
            yield "exp"
            S.op("dve", lambda e, eG=eG, sl=sl, qT=qT: e.tensor_tensor(out=qT[:, sl], in0=qa_all[:, sl], in1=eG,
                                                                       op=ALU.mult),
                 reads=[R_qa[b], ReG], writes=[R_HB[("q", par, b)]])
            S.op("dve", lambda e, enG=enG, sl=sl, kT=kT: e.scalar_tensor_tensor(
                out=kT[:, sl], in0=th_all[:, sl], scalar=1.0, in1=enG, op0=ALU.subtract, op1=ALU.mult),
                 reads=[R_th[b], Rlf], writes=[R_HB[("k", par, b)]])
            if b == 0:
                pieces = [(eG[:, 15:16], 0, 1), (eG[:, 16:32], 1, 16),
                          (eG[:, 32:BL].rearrange("p (c t) -> p c t", t=32)[:, :, 31], 17, 12)]
            else:
                pieces = [(eG.rearrange("p (c t) -> p c t", t=32)[:, :, 31], 17 + 13 * b - 1, 13)]
            for src, c0, n in pieces:
                S.op("dve", lambda e, src=src, c0=c0, n=n, par=par: e.tensor_copy(out=egl[par][:, c0:c0 + n],
                                                                                  in_=src),
                     reads=[ReG], writes=[R_egl[par][b]])
            if b == 0:
                S.op("dve", lambda e, par=par, kT=kT, khT=khT: e.tensor_tensor(
                    out=khT[:, 0:16], in0=kT[:, 0:16], in1=egl[par][:, 0:1].broadcast_to([128, 16]),
                    op=ALU.mult),
                     reads=[R_HB[("k", par, b)], R_egl[par][b]], writes=[R_HB[("kh", par, b)]])
                S.op("dve", lambda e, par=par, kT=kT, khT=khT: e.tensor_tensor(
                    out=khT[:, 16:32], in0=kT[:, 16:32], in1=egl[par][:, 1:17], op=ALU.mult),
                     reads=[R_HB[("k", par, b)], R_egl[par][b]], writes=[R_HB[("kh", par, b)]])
                lo, c0, n = 32, 17, 12
            else:
                lo, c0, n = b * BL, 17 + 13 * b - 1, 13
            S.op("dve", lambda e, par=par, lo=lo, c0=c0, n=n, kT=kT, khT=khT: e.tensor_tensor(
                out=khT[:, lo:lo + 32 * n].rearrange("p (c t) -> p c t", t=32),
                in0=kT[:, lo:lo + 32 * n].rearrange("p (c t) -> p c t", t=32),
                in1=egl[par][:, c0:c0 + n].unsqueeze(2).broadcast_to([128, n, 32]), op=ALU.mult),
                 reads=[R_HB[("k", par, b)], R_egl[par][b]], writes=[R_HB[("kh", par, b)]])
            yield "mul"

    def fillA(h):
        pa = projA(h)
        for gi, grp in enumerate(((0, 1, 2), (3, 4))):
            for b in grp:
                for _ in range(3):
                    next(pa)
                    yield "proj"
            for kind in (("q", "f"), ("z",))[gi]:
                issue_w(h + 1, kind)
            gens = [elemA(h, (b,), k_) for k_, b in enumerate(grp)]
            for stage in range(4):
                tag = None
                for g_ in gens:
                    tag = next(g_)
                    yield ("elem" if (tag == "exp" and g_ is gens[-1]) else "estage")
        for _ in pa:
            yield "v"
        issue_w(h + 1, "v")

    def pump(filler, n):
        for _ in range(n):
            try:
                next(filler)
            except StopIteration:
                return

    misc_rot = [0]

    def misc_slot():
        j = misc_rot[0] % 4
        misc_rot[0] += 1
        return j

    def hb_res(kind, par, t0, n):
        return [R_HB[(kind, par, b)] for b in blocks_of(t0, n)]

    onorm_ctr = [0]

    def onorm_stages(h, par, po, Rpo, t0, n, ssq_bank=None, ssq_col=0, bufidx=0):
        ob_ = bufidx
        ver = {}
        zres = hb_res("z", par, t0, n)
        ver["z"] = [r_.w for r_ in zres]
        osb_, Rosb = osb[ob_], R_osb[ob_]
        osq_, Rosq = osq[ob_], R_osq[ob_]
        rsd_, Rrsd = rsd[ob_], R_rsd[ob_]
        tno_, Rtno = tno[ob_], R_tno[ob_]
        sb_ = B_SSQ if ssq_bank is None else ssq_bank
        pss = pbank[sb_][:, ssq_col:ssq_col + n]

        def s1():
            S.op("act", lambda e: e.activation(out=osb_[:, 0:n], in_=po, func=AF.Copy), reads=Rpo, writes=[Rosb])
            S.op("act", lambda e: e.activation(out=osq_[:, 0:n], in_=osb_[:, 0:n], func=AF.Square),
                 reads=[Rosb], writes=[Rosq])
            ver["osb"], ver["osq"] = Rosb.w, Rosq.w

        def s2():
            assert Rosq.w == ver["osq"], "osq clobbered"
            S.op("pe", lambda e: e.matmul(pss, onesb[:], osq_[:, 0:n], start=True, stop=True),
                 reads=[Rosq, R_c["ones"]], writes=[R_pb[sb_]])
            ver["pss"] = R_pb[sb_].w

        def s3():
            assert R_pb[sb_].w == ver["pss"], "ssq clobbered"
            S.op("act", lambda e: e.activation(out=rsd_[:, 0:n], in_=pss, func=AF.Ln, scale=1.0 / 128,
                                               bias=epst[:, 0:1]),
                 reads=[R_pb[sb_], R_c["eps"]], writes=[Rrsd])
            S.op("act", lambda e: e.activation(out=rsd_[:, 0:n], in_=rsd_[:, 0:n], func=AF.Exp, scale=-0.5),
                 reads=[Rrsd], writes=[Rrsd])
            ver["rsd"] = Rrsd.w

        def s4():
            assert Rosb.w == ver["osb"] and Rrsd.w == ver["rsd"], "osb/rsd clobbered"
            S.op("dve", lambda e: e.scalar_tensor_tensor(out=tno_[:, 0:n], in0=osb_[:, 0:n], scalar=hn[:, 0:1],
                                                         in1=rsd_[:, 0:n], op0=ALU.mult, op1=ALU.mult),
                 reads=[Rosb, Rrsd, R_c["hn"]], writes=[Rtno])
            ver["tno"] = Rtno.w

        def s5():
            assert Rtno.w == ver["tno"], "tno clobbered"
            assert [r_.w for r_ in zres] == ver["z"], "z clobbered"
            S.op("dve", lambda e: e.tensor_tensor(
                out=oz[:, h, t0:t0 + n], in0=tno_[:, 0:n], in1=HB[("z", par)][:, t0:t0 + n], op=ALU.mult),
                 reads=[Rtno] + hb_res("z", par, t0, n), writes=[R_oz[h]])

        return s1, s2, s3, s4, s5

    def hgrn(h, filler=iter(()), carry=None, last=False):
        par = h % 2
        qT, kT, khT = HB[("q", par)], HB[("k", par)], HB[("kh", par)]
        pUb = [pbank[bu][:].rearrange("p (c v) -> p c v", v=128) for bu in B_U]
        sched = {}
        lnexp_pending = []
        if carry is not None:
            for it_c, items in carry[0].items():
                sched.setdefault(it_c, []).extend(items)
            lnexp_pending.extend(carry[1])

        def at(it, fn):
            sched.setdefault(it, []).append(fn)

        def sched_onorm(it, po, Rpo, t0, n, s2_at=None, **kw):
            st_ = onorm_stages(h, par, po, Rpo, t0, n, **kw)
            st_[0]()
            if s2_at is None:
                at(it + 1, ("onorm2", st_))
            else:
                at(s2_at, ("onorm2now", st_))

        def flush_lnexp(it, force):
            keep = []
            for item in lnexp_pending:
                due, st_ = item
                if force or it >= due:
                    st_[2]()
                    at(it + 2, st_[3])
                    at(it + 3, st_[4])
                else:
                    keep.append(item)
            lnexp_pending[:] = keep

        def run_item(it, item):
            if isinstance(item, tuple) and item[0] == "onorm2":
                item[1][1]()
                lnexp_pending.append((it + 1, item[1]))
                return None
            if isinstance(item, tuple) and item[0] == "onorm2now":
                item[1][1]()
                item[1][2]()
                at(it + 2, item[1][3])
                at(it + 3, item[1][4])
                return None
            return item()

        def pumpf(it):
            try:
                tag = next(filler)
            except StopIteration:
                return
            if tag == "elem":
                flush_lnexp(it, True)

        pk0 = pbank[B_KT][:].bitcast(BF16)[0:32, 0:128]
        S.op("pe", lambda e: e.transpose(pk0, khT[:, 0:32], ident[:]),
             reads=hb_res("kh", par, 0, 32) + [R_c["ident"]], writes=[R_pb[B_KT]])
        S.op("dve", lambda e: e.tensor_copy(out=ktm0[0:32, :], in_=pk0),
             reads=[R_pb[B_KT]], writes=[R_ktm0])
        BU1 = B_U[1]
        S.op("pe", lambda e: e.matmul(pUb[1][:, 0, :], ktm0[0:16, :], vtm[par][0:16, 0, :], start=True, stop=True),
             reads=[R_ktm0, R_vtm[par][0]], writes=[R_pb[BU1]])
        S.op("dve", lambda e: e.tensor_scalar(out=sall[1][:, 3, :], in0=pUb[1][:, 0, :], scalar1=-1.0,
                                              scalar2=None, op0=ALU.mult),
             reads=[R_pb[BU1]], writes=[R_sall[1][3]])
        S.op("act", lambda e: e.activation(out=send[2][:, :], in_=sall[1][:, 3, :], func=AF.Copy),
             reads=[R_sall[1][3]], writes=[R_send[2]])

        def sample_load():
            for hf in range(2):
                n0 = 8 * hf
                S.dma("sp", lambda e, hf=hf, n0=n0: e.dma_start(
                    out=ssm[hf][:], in_=state_hgrn[n0:n0 + 8, h, :, :].rearrange("n k v -> k n v")),
                      "ssm%d" % hf, writes=[R_ssm[hf]])

        def sample_cast(hf):
            S.op("act", lambda e: e.activation(out=ssb[hf][:], in_=ssm[hf][:], func=AF.Copy),
                 reads=[R_ssm[hf]], writes=[R_ssb[hf]])

        def sample_vmk(hf):
            n0 = 8 * hf
            S.op("pool", lambda e: e.tensor_tensor(
                out=vmk[hf][0:32, :, :], in0=vtm[par][0:32, 0:1, :].broadcast_to([32, 8, 128]),
                in1=smaskp[:, n0:n0 + 8].unsqueeze(2).broadcast_to([32, 8, 128]), op=ALU.mult),
                 reads=[R_vtm[par][0], R_c["smaskp"]], writes=[R_vm[hf]])

        def sample_scores():
            pss_ = pbank[B_SC][0:32, 0:16]
            S.op("pe", lambda e: e.matmul(pss_, kT[:, 0:32], qT[:, 16:32], start=True, stop=True),
                 reads=hb_res("k", par, 0, 32) + hb_res("q", par, 0, 32), writes=[R_pb[B_SC]])
            S.op("dve", lambda e: e.tensor_tensor(out=pm0[0:32, 0:16], in0=pss_, in1=smask[:], op=ALU.mult),
                 reads=[R_pb[B_SC], R_c["smask"]], writes=[R_pm0])

        def sample_round(r, ub):
            hf, q4 = r // 2, r % 2
            n0 = 8 * hf
            bu = B_SC
            S.op("pe", lambda e: e.matmul(
                pbank[bu][:, :], ktm0[0:32, :],
                vmk[hf][0:32, 4 * q4:4 * q4 + 4, :].rearrange("p n v -> p (n v)"), start=True, stop=True),
                 reads=[R_ktm0, R_vm[hf]], writes=[R_pb[bu]])
            for n_ in range(4):
                nn = 4 * q4 + n_
                S.op("dve", lambda e, nn=nn, n_=n_: e.scalar_tensor_tensor(
                    out=ssm[hf][:, nn, :], in0=ssm[hf][:, nn, :], scalar=egl[par][:, 1 + n0 + nn:2 + n0 + nn],
                    in1=pbank[B_SC][:, n_ * 128:(n_ + 1) * 128], op0=ALU.mult, op1=ALU.subtract),
                     reads=[R_ssm[hf], R_egl[par][0], R_pb[bu]], writes=[R_ssm[hf]])
            if q4 == 1:
                out_events.append(S.dma("sp", lambda e: e.dma_start(
                    out=nh_sample[n0:n0 + 8, h, :, :].rearrange("n k v -> k n v"), in_=ssm[hf][:]),
                    "ssm%d" % hf, reads=[R_ssm[hf]]))

        def sample_oT(it):
            pos = pbank[B_OT][:, 0:16]
            S.op("pe", lambda e: e.matmul(pos, vtm[par][0:32, 0, :], pm0[0:32, 0:16], start=True, stop=False),
                 reads=[R_vtm[par][0], R_pm0], writes=[R_pb[B_OT]])
            for n_ in range(NS):
                hf = n_ // 8
                S.op("pe", lambda e, n_=n_, hf=hf: e.matmul(pos[:, n_:n_ + 1], ssb[hf][:, n_ % 8, :],
                                                          qT[:, 16 + n_:17 + n_], start=False, stop=(n_ == NS - 1)),
                     reads=[R_ssb[hf]] + hb_res("q", par, 0, 32), writes=[R_pb[B_OT]])
            sched_onorm(it, pos, [R_pb[B_OT]], 16, 16, s2_at=13, bufidx=2)

        def quad_kt_pe(quad):
            tq = 32 + 512 * quad
            pk = pbank[B_KT][:].bitcast(BF16)[:, 0:512].rearrange("p (g k) -> p g k", k=128)
            for gq in range(4):
                S.op("pe", lambda e, gq=gq: e.transpose(pk[:, gq, :], khT[:, tq + 128 * gq:tq + 128 * gq + 128],
                                                        ident[:]),
                     reads=hb_res("kh", par, tq + 128 * gq, 128) + [R_c["ident"]], writes=[R_pb[B_KT]])

        def quad_kt_act(quad):
            qp = quad % 2
            pk = pbank[B_KT][:].bitcast(BF16)[:, 0:512].rearrange("p (g k) -> p g k", k=128)
            S.op("dve", lambda e: e.tensor_copy(out=ktm4[qp][:, :, :], in_=pk),
                 reads=[R_pb[B_KT]], writes=[R_ktm4[qp]])

        def quad_sc_pe(quad):
            tq = 32 + 512 * quad
            psc = pbank[B_SC][:, :].rearrange("p (g t) -> p g t", t=128)
            for gq in range(4):
                t0 = tq + 128 * gq
                S.op("pe", lambda e, gq=gq, t0=t0: e.matmul(psc[:, gq, :], kT[:, t0:t0 + 128], qT[:, t0:t0 + 128],
                                                           start=True, stop=True),
                     reads=hb_res("k", par, t0, 128) + hb_res("q", par, t0, 128), writes=[R_pb[B_SC]])

        def quad_mask_dve(quad):
            qp = quad % 2
            psc = pbank[B_SC][:, :].rearrange("p (g t) -> p g t", t=128)
            S.op("dve", lambda e: e.tensor_tensor(
                out=pm4[qp][:, :].rearrange("p (g t) -> p g t", t=128), in0=psc,
                in1=cmask[:].unsqueeze(1).broadcast_to([128, 4, 128]), op=ALU.mult),
                 reads=[R_pb[B_SC], R_c["cmask"]], writes=[R_pm4[qp]])

        def emit_quad(quad):
            quad_kt_pe(quad)
            quad_kt_act(quad)
            quad_sc_pe(quad)
            quad_mask_dve(quad)

        def emit_vm(g):
            vb = g % 4
            S.op("pool", lambda e: e.tensor_tensor(
                out=vm4[vb][:, :].rearrange("p (c v) -> p c v", v=128),
                in0=vtm[par][:, g + 1:g + 2, :].broadcast_to([128, 4, 128]),
                in1=bmask[:].unsqueeze(2).broadcast_to([128, 4, 128]), op=ALU.mult),
                 reads=[R_vtm[par][g + 1], R_c["smaskp"]], writes=[R_vm4[vb]])

        def emit_U(g):
            quad, gq = g // 4, g % 4
            qp, vb, bu = quad % 2, g % 4, B_U[g % 2]
            S.op("pe", lambda e: e.matmul(pbank[bu][:, :], ktm4[qp][:, gq, :], vm4[vb][:, :], start=True, stop=True),
                 reads=[R_ktm4[qp], R_vm4[vb]], writes=[R_pb[bu]])

        def emit_chain(g):
            gp = g % 2
            bu = B_U[g % 2]
            pUg = pUb[g % 2]
            for c in range(4):
                if c == 0:
                    src, Rsrc = sall[1 - gp][:, 3, :], R_sall[1 - gp][3]
                else:
                    src, Rsrc = sall[gp][:, c - 1, :], R_sall[gp][c - 1]
                col = 17 + 4 * g + c
                S.op("dve", lambda e, c=c, src=src, col=col: e.scalar_tensor_tensor(
                    out=sall[gp][:, c, :], in0=src, scalar=egl[par][:, col:col + 1], in1=pUg[:, c, :],
                    op0=ALU.mult, op1=ALU.subtract),
                     reads=[Rsrc, R_pb[bu]] + [R_egl[par][b] for b in range(NBLK)], writes=[R_sall[gp][c]])
            if DBG_S is not None:
                out_events.append(S.dma("sp", lambda e: e.dma_start(
                    out=DBG_S[h, 4 * g:4 * g + 4, :, :].rearrange("c k v -> k c v"), in_=sall[gp][:, :, :]),
                    "o_dbgs%d" % gp, reads=R_sall[gp]))

        def emit_casts(g):
            gp = g % 2
            S.op("act", lambda e: e.activation(out=sbb[gp][:, 0:3, :], in_=sall[gp][:, 0:3, :], func=AF.Copy),
                 reads=R_sall[gp][0:3], writes=R_sbb[gp][0:3])
            S.op("act", lambda e: e.activation(out=send[g % 3][:, :], in_=sall[gp][:, 3, :], func=AF.Copy),
                 reads=[R_sall[gp][3]], writes=[R_send[g % 3]])

        def emit_oT(g, it):
            gp = g % 2
            t0 = 32 + 128 * g
            quad, gq = g // 4, g % 4
            ob = B_OT
            qp = quad % 2
            po = pbank[ob][:, gq * 128:(gq + 1) * 128]
            S.op("pe", lambda e: e.matmul(po, vtm[par][:, g + 1, :], pm4[qp][:, gq * 128:(gq + 1) * 128],
                                          start=True, stop=False),
                 reads=[R_vtm[par][g + 1], R_pm4[qp]], writes=[R_pb[ob]])
            for c in range(4):
                if c == 0:
                    sbv, Rsb = send[(g - 1) % 3][:, :], R_send[(g - 1) % 3]
                else:
                    sbv, Rsb = sbb[gp][:, c - 1, :], R_sbb[gp][c - 1]
                S.op("pe", lambda e, c=c, sbv=sbv: e.matmul(
                    po[:, 32 * c:32 * c + 32], sbv, qT[:, t0 + 32 * c:t0 + 32 * c + 32], start=False,
                    stop=(c == 3)),
                     reads=[Rsb] + hb_res("q", par, t0, 128), writes=[R_pb[ob]])
            if gq == 3:
                sched_onorm(it, pbank[ob][:, :], [R_pb[ob]], 32 + 512 * quad, 512, bufidx=quad % 2)
                if sample_pending and quad == 1:
                    sample_pending.pop()
                    sample_oT(it)

        sample_pending = [True]
        at(0, sample_load)
        at(2, lambda: sample_cast(0))
        at(3, lambda: sample_vmk(0))
        at(5, lambda: sample_cast(1))
        at(6, lambda: sample_vmk(1))
        at(4, sample_scores)
        for r, it_r in enumerate((12, 13, 14, 15)):
            at(it_r, (lambda r=r: ("round", r)))
        emit_quad(0)
        emit_vm(0)
        emit_vm(1)
        emit_U(0)
        NIT = NGRP + 2
        for it in range(NIT):
            rounds = []
            for item in sched.pop(it, []):
                res = run_item(it, item)
                if isinstance(res, tuple) and res[0] == "round":
                    rounds.append(res[1])
            if it < NGRP:
                g = it
                if g + 2 < NGRP:
                    emit_vm(g + 2)
                if (g + 4) % 4 == 0 and (g + 4) // 4 < 4:
                    quad_kt_pe((g + 4) // 4)
                if (g + 3) % 4 == 0 and (g + 3) // 4 < 4:
                    quad_kt_act((g + 3) // 4)
                if (g + 2) % 4 == 0 and (g + 2) // 4 < 4:
                    quad_sc_pe((g + 2) // 4)
                if (g + 1) % 4 == 0 and (g + 1) // 4 < 4:
                    quad_mask_dve((g + 1) // 4)
                if g + 1 < NGRP:
                    emit_U(g + 1)
                emit_chain(g)
            for r in rounds:
                sample_round(r, it % 2)
            if 0 <= it - 1 < NGRP:
                emit_casts(it - 1)
            pumpf(it)
            if 0 <= it - 2 < NGRP:
                emit_oT(it - 2, it)
            pumpf(it)
            if it % 4 == 1:
                pumpf(it)
            flush_lnexp(it, False)
        out_events.append(S.dma("sp", lambda e: e.dma_start(out=nh_prompt[h, :, :], in_=sall[1][:, 3, :]),
                                "o_nhp", reads=[R_sall[1][3]]))
        for _ in filler:
            pass
        if last:
            it = NIT
            while sched or lnexp_pending:
                for item in sched.pop(it, []):
                    run_item(it, item)
                flush_lnexp(it, False)
                it += 1
                assert it < NIT + 20
            return None
        carry_s = {}
        for it_c, items in sched.items():
            assert it_c >= NIT
            carry_s[it_c - NIT] = items
        carry_l = [(due - NIT, st_) for due, st_ in lnexp_pending]
        return carry_s, carry_l

    B_BANKS = [0, 1, 2, 3, 4, 5, 6, 7]

    def phaseB(j, banks=None):
        banks = B_BANKS if banks is None else banks
        issue_B(j, ("z", "c", "h", "b"))
        wz_, Rwz = wcache[("B", j, "z")]
        wc_, Rwc = wcache[("B", j, "c")]
        wh_, Rwh = wcache[("B", j, "h")]
        wb_, Rwb = wcache[("B", j, "b")]
        for b in range(NBLK):
            sl = slice(b * BL, (b + 1) * BL)
            sp_ = b % 2
            szb, Rszb = slot_f(0 + sp_, BL), R_slot[0 + sp_]
            csb, Rcsb = slot_f(2 + sp_, BL), R_slot[2 + sp_]
            ub, Rub = slot_f(4 + sp_, BL + 2), R_slot[4 + sp_]
            t1, Rt1 = slot_f(6 + sp_, BL), R_slot[6 + sp_]
            t2, Rt2 = slot_f(8 + sp_, BL), R_slot[8 + sp_]
            pz, Rpz = fm_proj(wz_, Rwz, b, banks)
            S.op("act", lambda e, pz=pz, szb=szb: e.activation(out=szb, in_=pz, func=AF.Silu),
                 reads=[Rpz], writes=[Rszb])
            yield "proj"
            pc, Rpc = fm_proj(wc_, Rwc, b, banks)
            S.op("act", lambda e, pc=pc, csb=csb: e.activation(out=csb, in_=pc, func=AF.Copy),
                 reads=[Rpc], writes=[Rcsb])
            yield "proj"
            ph, Rph = fm_proj(wh_, Rwh, b, banks)
            S.op("dve", lambda e, ph=ph, csb=csb, ub=ub: e.tensor_tensor(out=ub[:, 2:2 + BL], in0=csb, in1=ph,
                                                                         op=ALU.mult),
                 reads=[Rcsb, Rph], writes=[Rub])
            yield "proj"
            pb_, Rpb_ = fm_proj(wb_, Rwb, b, banks)
            if b == 0:
                S.op("dve", lambda e, ub=ub, j=j: e.tensor_copy(out=uc[:, j, 0:16], in_=ub[:, 2 + 16:2 + 32]),
                     reads=[Rub], writes=[R_c["uc"]])
                S.op("dve", lambda e, j=j, t1=t1: e.tensor_scalar(out=t1[:, 16:32], in0=ctxT[:, 0, j, :],
                                                                  scalar1=cwt[:, 0, j:j + 1], scalar2=None,
                                                                  op0=ALU.mult),
                     reads=[R_c["ctxT"], R_c["cwt"]], writes=[Rt1])
                S.op("dve", lambda e, j=j, t1=t1: e.scalar_tensor_tensor(
                    out=t1[:, 16:32], in0=ctxT[:, 1, j, :], scalar=cwt[:, 1, j:j + 1], in1=t1[:, 16:32],
                    op0=ALU.mult, op1=ALU.add),
                     reads=[R_c["ctxT"], R_c["cwt"], Rt1], writes=[Rt1])
                S.op("dve", lambda e, j=j, t1=t1, t2=t2, ub=ub: e.scalar_tensor_tensor(
                    out=t2[:, 16:32], in0=ub[:, 2 + 16:2 + 32], scalar=cwt[:, 2, j:j + 1], in1=t1[:, 16:32],
                    op0=ALU.mult, op1=ALU.add),
                     reads=[Rub, R_c["cwt"], Rt1], writes=[Rt2])
                S.op("dve", lambda e, ub=ub: e.tensor_copy(out=ub[:, 2 + 30:2 + 32], in_=ub[:, 2 + 14:2 + 16]),
                     reads=[Rub], writes=[Rub])
                lo = 32
            else:
                prev = slot_f(4 + (1 - sp_), BL + 2)
                S.op("dve", lambda e, ub=ub, prev=prev: e.tensor_copy(out=ub[:, 0:2], in_=prev[:, BL:BL + 2]),
                     reads=[R_slot[4 + (1 - sp_)]], writes=[Rub])
                lo = 0
            n = BL - lo
            S.op("dve", lambda e, ub=ub, t1=t1, lo=lo, n=n, j=j: e.tensor_scalar(
                out=t1[:, lo:lo + n], in0=ub[:, lo:lo + n], scalar1=cwt[:, 0, j:j + 1], scalar2=None, op0=ALU.mult),
                 reads=[Rub, R_c["cwt"]], writes=[Rt1])
            S.op("dve", lambda e, ub=ub, t1=t1, lo=lo, n=n, j=j: e.scalar_tensor_tensor(
                out=t1[:, lo:lo + n], in0=ub[:, lo + 1:lo + 1 + n], scalar=cwt[:, 1, j:j + 1], in1=t1[:, lo:lo + n],
                op0=ALU.mult, op1=ALU.add),
                 reads=[Rub, R_c["cwt"], Rt1], writes=[Rt1])
            S.op("dve", lambda e, ub=ub, t1=t1, t2=t2, lo=lo, n=n, j=j: e.scalar_tensor_tensor(
                out=t2[:, lo:lo + n], in0=ub[:, lo + 2:lo + 2 + n], scalar=cwt[:, 2, j:j + 1], in1=t1[:, lo:lo + n],
                op0=ALU.mult, op1=ALU.add),
                 reads=[Rub, R_c["cwt"], Rt1], writes=[Rt2])
            lo2 = 16 if b == 0 else 0
            n2 = BL - lo2
            S.op("dve", lambda e, pb_=pb_, t2=t2, lo2=lo2, n2=n2: e.tensor_tensor(
                out=t2[:, lo2:lo2 + n2], in0=t2[:, lo2:lo2 + n2], in1=pb_[:, lo2:lo2 + n2], op=ALU.mult),
                 reads=[Rt2, Rpb_], writes=[Rt2])
            S.op("dve", lambda e, t2=t2, szb=szb, lo2=lo2, n2=n2, j=j, b=b: e.tensor_tensor(
                out=ybin[:, j, b * BL + lo2:b * BL + lo2 + n2], in0=t2[:, lo2:lo2 + n2],
                in1=szb[:, lo2:lo2 + n2], op=ALU.mult),
                 reads=[Rt2, Rszb], writes=[R_ybin[j]])
            if b == NBLK - 1:
                S.op("dve", lambda e, ub=ub, j=j: e.tensor_copy(out=uc[:, j, 16:18], in_=ub[:, BL:BL + 2]),
                     reads=[Rub], writes=[R_c["uc"]])
            if b == 1:
                if j + 1 < 8:
                    issue_B(j + 1, ("z", "c", "h"))
                else:
                    issue_M(0, ("ga", "gb", "wa"))
            yield "proj"

    def conv_out():
        for half in range(2):
            pt = pbank[half][0:18, :].rearrange("p (c v) -> p c v", v=128)
            for c4 in range(4):
                ct = half * 4 + c4
                S.op("pe", lambda e, pt=pt, c4=c4, ct=ct: e.transpose(pt[:, c4, :], uc[:, ct, :], identf[:]),
                     reads=[R_c["uc"], R_c["identf"]], writes=[R_pb[half]])
            S.op("dve", lambda e, half=half: e.tensor_copy(out=ucT[:, half * 512:(half + 1) * 512],
                                                           in_=pbank[half][0:18, :]),
                 reads=[R_pb[half]], writes=R_slot[10:12])
        out_events.append(S.dma("sp", lambda e: e.dma_start(out=nc_sample[:, 1, :], in_=ucT[0:16, :]), "o_ncs1",
                                reads=R_slot[10:12]))
        out_events.append(S.dma("sp", lambda e: e.dma_start(out=nc_prompt[:, :], in_=ucT[16:18, :]), "o_ncp",
                                reads=R_slot[10:12]))

    def fm_acc(wt, Rw, src, Rsrc, b, banks):
        bi = banks[pb_rot[0] % len(banks)]
        pb_rot[0] += 1
        pt = pbank[bi][:, 0:BL]
        for c in range(8):
            S.op("pe", lambda e, c=c, pt=pt, wt=wt, b=b, src=src: e.matmul(pt, wt[:, c, :],
                                                                         src[:, c, b * BL:(b + 1) * BL],
                                                                         start=(c == 0), stop=(c == 7)),
                 reads=[Rw, Rsrc[c]], writes=[R_pb[bi]])
        return pt, R_pb[bi]

    WO_OFF = 8 * T
    wo = bass.AP(r2, WO_OFF, [[R2_BYTES // 2, 128], [D, 8], [1, D]])
    R_wo = Res("wo")

    def load_wo():
        R_wo.inherit(*a2_res)
        S.dma("pool", lambda e: [e.dma_start(out=wo[:, c, :],
                                             in_=w_o.rearrange("(k p) n -> p k n", p=128)[:, c, :])
                                 for c in range(8)], "wo", writes=[R_wo], n=8)

    def phaseM(j):
        issue_M(j, ("ga", "gb", "wa", "wb"))
        wga, Rwga = wcache[("M", j, "ga")]
        wgb, Rwgb = wcache[("M", j, "gb")]
        wa_, Rwa = wcache[("M", j, "wa")]
        wb2, Rwb2 = wcache[("M", j, "wb")]
        if j == 0:
            load_wo()
        mbuf = merged_bufs[j]
        for b in range(NBLK):
            sl = slice(b * BL, (b + 1) * BL)
            sp_ = b % 2
            tha, Rtha = slot_f(0 + sp_, BL), R_slot[0 + sp_]
            thb, Rthb = slot_f(2 + sp_, BL), R_slot[2 + sp_]
            m1, Rm1 = slot_f(4 + sp_, BL), R_slot[4 + sp_]
            m2, Rm2 = slot_f(6 + sp_, BL), R_slot[6 + sp_]
            pga, Rpga = fm_proj(wga, Rwga, b, B_BANKS)
            S.op("act", lambda e, pga=pga, tha=tha: e.activation(out=tha, in_=pga, func=AF.Tanh, scale=0.5),
                 reads=[Rpga], writes=[Rtha])
            pgb, Rpgb = fm_proj(wgb, Rwgb, b, B_BANKS)
            S.op("act", lambda e, pgb=pgb, thb=thb: e.activation(out=thb, in_=pgb, func=AF.Tanh, scale=0.5),
                 reads=[Rpgb], writes=[Rthb])
            pya, Rpya = fm_acc(wa_, Rwa, oz, R_oz, b, B_BANKS)
            S.op("dve", lambda e, tha=tha, pya=pya, m1=m1: e.scalar_tensor_tensor(
                out=m1, in0=tha, scalar=1.0, in1=pya, op0=ALU.add, op1=ALU.mult),
                 reads=[Rtha, Rpya], writes=[Rm1])
            pyb, Rpyb = fm_acc(wb2, Rwb2, ybin, R_ybin, b, B_BANKS)
            S.op("dve", lambda e, thb=thb, pyb=pyb, m2=m2: e.scalar_tensor_tensor(
                out=m2, in0=thb, scalar=1.0, in1=pyb, op0=ALU.add, op1=ALU.mult),
                 reads=[Rthb, Rpyb], writes=[Rm2])
            S.op("dve", lambda e, m1=m1, m2=m2, sl=sl, mbuf=mbuf: e.tensor_tensor(out=mbuf[:, sl], in0=m1, in1=m2,
                                                                               op=ALU.add),
                 reads=[Rm1, Rm2], writes=[merged_res[j][b]])
            if b == 1 and j + 1 < 8:
                issue_M(j + 1, ("ga", "gb", "wa"))

    def phaseF():
        npost_ap = r3[:, 12 * 512:14 * 512]
        S.dma("sp", lambda e: e.dma_start(out=npost_ap, in_=norm_post.broadcast_to([128, D])), "c_npost",
              writes=[R_slot[12], R_slot[13]])
        for i in list(range(1, 17)) + [0]:
            t0, P = tm_tokens(i)
            par = i % 2
            xsl = (0, 2, 10)[i % 3]
            xs = r3[:, xsl * 512:(xsl + 2) * 512]
            Rx = [R_slot[xsl], R_slot[xsl + 1]]
            ys = r3[:, (4 + par * 2) * 512:(4 + par * 2 + 2) * 512]
            Ry = [R_slot[4 + par * 2], R_slot[4 + par * 2 + 1]]
            junk = r3[:, 8 * 512:10 * 512]
            Rj = [R_slot[8], R_slot[9]]
            if i == 0:
                S.dma("pool", lambda e, xs=xs: [e.dma_start(out=xs[0:NMETA, :], in_=meta_tokens),
                                                e.dma_start(out=xs[NMETA:32, :], in_=x_sample)],
                      "xf%d" % (i % 3), writes=Rx, n=2)
            else:
                S.dma("pool", lambda e, xs=xs, i=i: e.dma_start(out=xs[:, :], in_=x_prompt[128 * (i - 1):128 * i, :]),
                      "xf%d" % (i % 3), writes=Rx)
            banks = (2 * (i % 4), 2 * (i % 4) + 1)
            sc0 = 8 * (4 + i % 4) - 32 if False else 8 * (i % 4)
            Rsm = R_small[4 + i % 4]
            sc0 = 32 + 8 * (i % 4)
            Rm_all = [merged_res[c][b] for c in range(8) for b in blocks_of(t0, P)]
            for half in range(2):
                po = pbank[banks[half]][0:P, :]
                for c in range(8):
                    S.op("pe", lambda e, po=po, c=c, t0=t0, P=P, half=half: e.matmul(
                        po, merged_bufs[c][:, t0:t0 + P], wo[:, c, half * 512:(half + 1) * 512],
                        start=(c == 0), stop=(c == 7)),
                         reads=[R_wo] + [merged_res[c][b] for b in blocks_of(t0, P)], writes=[R_pb[banks[half]]])
            for half in range(2):
                po = pbank[banks[half]][0:P, :]
                S.op("act", lambda e, po=po, P=P, half=half, junk=junk, sc0=sc0: e.activation(
                    out=junk[0:P, 0:512], in_=po, func=AF.Square, accum_out=small[0:P, sc0 + 4 + half:sc0 + 5 + half]),
                     reads=[R_pb[banks[half]]], writes=Rj + [Rsm])
            S.op("act", lambda e, P=P, sc0=sc0: e.activation(out=small[0:P, sc0 + 6:sc0 + 7],
                                                             in_=small[0:P, sc0 + 4:sc0 + 5], func=AF.Identity,
                                                             bias=small[0:P, sc0 + 5:sc0 + 6]),
                 reads=[Rsm], writes=[Rsm])
            S.op("act", lambda e, P=P, sc0=sc0: e.activation(out=small[0:P, sc0 + 1:sc0 + 2], in_=small[0:P, sc0 + 6:sc0 + 7], func=AF.Ln,
                                                    scale=1.0 / D, bias=epst[0:P, 1:2]),
                 reads=[Rsm, R_c["eps"]], writes=[Rsm])
            S.op("act", lambda e, P=P, sc0=sc0: e.activation(out=small[0:P, sc0 + 2:sc0 + 3], in_=small[0:P, sc0 + 1:sc0 + 2], func=AF.Exp,
                                                    scale=-0.5),
                 reads=[Rsm], writes=[Rsm])
            for half in range(2):
                po = pbank[banks[half]][0:P, :]
                hs = slice(half * 512, (half + 1) * 512)
                S.op("dve", lambda e, po=po, P=P, hs=hs, ys=ys, sc0=sc0: e.scalar_tensor_tensor(
                    out=ys[0:P, hs], in0=po, scalar=small[0:P, sc0 + 2:sc0 + 3], in1=npost_ap[0:P, hs], op0=ALU.mult,
                    op1=ALU.mult),
                     reads=[R_pb[banks[half]], Rsm, R_slot[12], R_slot[13]], writes=Ry)
            S.op("dve", lambda e, ys=ys, xs=xs, P=P: e.tensor_tensor(out=ys[0:P, :], in0=ys[0:P, :],
                                                                     in1=xs[0:P, :], op=ALU.add),
                 reads=Ry + Rx, writes=Ry)
            if i == 0:
                out_events.append(S.dma("sp", lambda e, ys=ys: e.dma_start(out=y_sample[:, :], in_=ys[NMETA:32, :]),
                                        "ys%d" % par, reads=Ry))
            else:
                out_events.append(S.dma("sp", lambda e, ys=ys, i=i: e.dma_start(
                    out=y_prompt[128 * (i - 1):128 * i, :], in_=ys[:, :]), "ys%d" % par, reads=Ry))

    import os as _os
    KSTOP = _os.environ.get("KSTOP", "")

    def record():
        setup()
        if KSTOP == "setup":
            return
        phase1()
        setup_late()
        if KSTOP == "phase1":
            return
        for b in range(NBLK):
            R_th[b].inherit(*R_slot[0:5])
            R_qa[b].inherit(*R_slot[5:8])
        nh_lim = int(KSTOP[1:]) if KSTOP.startswith("A") else NH
        f0 = fillA(0)
        for _ in f0:
            pass
        full = (nh_lim == NH) and not KSTOP.startswith("A")
        b0_started = False
        carry_ = None
        for h in range(nh_lim):
            if h + 1 < nh_lim:
                filler = fillA(h + 1)
            elif full:
                for i in range(8):
                    R_slot[i].inherit(*R_th, *R_qa)
                R_ybin[0].inherit(*R_vtm[0])
                S.op("pool", lambda e: e.memset(ybin[:, 0, 0:NMETA], 0.0), writes=[R_ybin[0]])
                filler = phaseB(0, A_BANKS_FM)
                b0_started = True
            else:
                filler = iter(())
            carry_ = hgrn(h, filler, carry_, last=(h == nh_lim - 1))
        if KSTOP.startswith("A"):
            return
        if not b0_started:
            for i in range(8):
                R_slot[i].inherit(*R_th, *R_qa)
        j0 = 1 if b0_started else 0
        for j in range(j0, 8):
            R_ybin[j].inherit(*a2_res)
        S.op("pool", lambda e: e.memset(ybin[:, j0:8, 0:NMETA], 0.0), writes=R_ybin[j0:8])
        for j in range(j0, 8):
            for _ in phaseB(j):
                pass
        conv_out()
        if KSTOP == "B":
            return
        for j in range(8):
            phaseM(j)
        if KSTOP == "M":
            return
        phaseF()

    record()
    S.emit(st, final_waits=out_events)
    return nc, S


_CACHE = {}


def kernel(x_prompt, x_sample, state_hgrn, state_conv, meta_tokens, w_in, norm_pre, norm_post, lb_logits,
           hgrn_norm, conv_w, w_a, w_b, w_o):
    f = lambda a: np.ascontiguousarray(np.asarray(a, dtype=np.float32))
    x_prompt, x_sample, state_hgrn, state_conv = f(x_prompt), f(x_sample), f(state_hgrn), f(state_conv)
    shared = {
        "meta_tokens": f(meta_tokens), "w_in": f(w_in)[0], "norm_pre": f(norm_pre), "norm_post": f(norm_post),
        "lb_logits": f(lb_logits), "hgrn_norm": f(hgrn_norm), "conv_w": f(conv_w)[0], "w_a": f(w_a)[0],
        "w_b": f(w_b)[0], "w_o": f(w_o)[0],
    }
    in_maps = []
    for c in range(NCORES):
        m = dict(shared)
        m["x_prompt"] = x_prompt[c]
        m["x_sample"] = np.ascontiguousarray(x_sample[NS * c:NS * (c + 1), 0, :])
        m["state_hgrn"] = np.ascontiguousarray(state_hgrn[0, NS * c:NS * (c + 1)])
        m["state_conv"] = np.ascontiguousarray(state_conv[0, NS * c:NS * (c + 1)])
        in_maps.append(m)
    if "nc" not in _CACHE:
        _CACHE["nc"] = build_program()[0]
    nc = _CACHE["nc"]
    res = run_bass_kernel_spmd(nc, in_maps, core_ids=list(range(NCORES)))
    r = res.results
    y_prompt = np.stack([r[c]["y_prompt"] for c in range(NCORES)], 0)
    y_sample = np.concatenate([r[c]["y_sample"] for c in range(NCORES)], 0)[:, None, :]
    nh_p = np.stack([r[c]["nh_prompt"] for c in range(NCORES)], 0)[None]
    nh_s = np.concatenate([r[c]["nh_sample"] for c in range(NCORES)], 0)[None]
    nc_p = np.stack([r[c]["nc_prompt"] for c in range(NCORES)], 0)[None]
    nc_s = np.concatenate([r[c]["nc_sample"] for c in range(NCORES)], 0)[None]
    return (y_prompt.astype(np.float32), y_sample.astype(np.float32), nh_p.astype(np.float32),
            nh_s.astype(np.float32), nc_p.astype(np.float32), nc_s.astype(np.float32))
```
